# Optimizing a Trainium2 kernel written in Bass

```python
import jax, jax.numpy as jnp
from jax import lax
import numpy as np

D_MODEL = 4096
BATCH = 4
SEQ = 4096
DEPTH = 1

MIX_WIDTH = D_MODEL
HEAD_DIM = 128
A_WIDTH = MIX_WIDTH // 2
A_HEADS = A_WIDTH // HEAD_DIM
A_KEY_DIM = 128
A_KEY_WIDTH = A_HEADS * A_KEY_DIM
B_WIDTH = MIX_WIDTH - A_WIDTH
B_GROUP_DIM = 128
B_GROUPS = B_WIDTH // B_GROUP_DIM
GMLP_CHUNK = 128
GLA_CHUNK = 64
IN_COLS = 2 * A_KEY_WIDTH + 2 * A_WIDTH + 2 * B_WIDTH
D_FF = ((8 * D_MODEL + 3 * 256 - 1) // (3 * 256)) * 256
PLE_DIM = 256
EPS = 1e-6

kernel_name = "hymba_hgrn2_gmlp_hybrid"


def _rmsnorm(x, w):
    xf = x.astype(jnp.float32)
    y = xf * lax.rsqrt(jnp.mean(xf * xf, axis=-1, keepdims=True) + EPS)
    return (y * w.astype(jnp.float32)).astype(x.dtype)


def _hgrn2(q, f_pre, i_in, g, lb, norm_w):
    bsz, t, _ = q.shape
    n = t // GLA_CHUNK
    f32 = jnp.float32
    lbf = lb.astype(f32)
    qf = jax.nn.silu(q.astype(f32))
    f = lbf + (1.0 - lbf) * jax.nn.sigmoid(f_pre.astype(f32))
    kf = 1.0 - f
    logf = jnp.log(jnp.maximum(f, 1e-30))

    def to_chunks(a, d):
        return a.reshape(bsz, n, GLA_CHUNK, A_HEADS, d).transpose(1, 0, 3, 2, 4)

    qc = to_chunks(qf, A_KEY_DIM)
    kc = to_chunks(kf, A_KEY_DIM)
    vc = to_chunks(i_in.astype(f32), HEAD_DIM)
    bc = jnp.cumsum(to_chunks(logf, A_KEY_DIM), axis=3)
    causal = jnp.tril(jnp.ones((GLA_CHUNK, GLA_CHUNK), dtype=bool))[:, :, None]

    def step(state, inp):
        q_c, k_c, v_c, b_c = inp
        inter = jnp.einsum('bhtk,bhkv->bhtv', q_c * jnp.exp(b_c), state)
        diff = b_c[:, :, :, None, :] - b_c[:, :, None, :, :]
        decay = jnp.exp(jnp.where(causal, diff, -jnp.inf))
        scores = jnp.einsum('bhtk,bhsk,bhtsk->bhts', q_c, k_c, decay)
        intra = jnp.einsum('bhts,bhsv->bhtv', scores, v_c)
        b_end = b_c[:, :, -1:, :]
        new_state = (jnp.exp(b_end[:, :, 0, :])[..., None] * state
                     + jnp.einsum('bhsk,bhsv->bhkv', k_c * jnp.exp(b_end - b_c), v_c))
        return new_state, inter + intra

    s0 = jnp.zeros((bsz, A_HEADS, A_KEY_DIM, HEAD_DIM), f32)
    _, o = lax.scan(step, s0, (qc, kc, vc, bc))
    o = o.transpose(1, 0, 3, 2, 4).reshape(bsz, t, A_HEADS, HEAD_DIM)
    o = o * lax.rsqrt(jnp.mean(o * o, axis=-1, keepdims=True) + EPS)
    o = o.reshape(bsz, t, A_WIDTH) * norm_w.astype(f32) * jax.nn.silu(g.astype(f32))
    return o.astype(q.dtype)


def _gmlp(u, v, ln_w, ln_b, w_s, b_s):
    bsz, t, _ = u.shape
    n = t // GMLP_CHUNK
    f32 = jnp.float32
    uf = jax.nn.gelu(u.astype(f32), approximate=False)
    vf = jax.nn.gelu(v.astype(f32), approximate=False)
    mu = jnp.mean(vf, axis=-1, keepdims=True)
    var = jnp.mean(jnp.square(vf - mu), axis=-1, keepdims=True)
    vf = (vf - mu) * lax.rsqrt(var + EPS) * ln_w.astype(f32) + ln_b.astype(f32)
    vc = vf.reshape(bsz, n, GMLP_CHUNK, B_GROUPS, B_GROUP_DIM)
    tril = jnp.tril(jnp.ones((GMLP_CHUNK, GMLP_CHUNK), f32))
    w = w_s.astype(f32) * tril
    z = jnp.einsum('gts,bnsgd->bntgd', w, vc) + b_s.astype(f32).T[None, None, :, :, None]
    return (uf * z.reshape(bsz, t, B_WIDTH)).astype(u.dtype)


def setup_inputs(seed: int = 0) -> dict:
    key = jax.random.key(seed)
    ks = jax.random.split(key, 20)
    nrm = jax.random.normal
    f32 = jnp.float32

    def gain(k, shape):
        return 1.0 + 0.01 * nrm(k, shape, f32)

    return {
        "x": nrm(ks[0], (BATCH, SEQ, D_MODEL), f32),
        "p": nrm(ks[1], (DEPTH, BATCH, SEQ, PLE_DIM), f32),
        "pre_mix_w": gain(ks[2], (DEPTH, D_MODEL)),
        "w_in": nrm(ks[3], (DEPTH, D_MODEL, IN_COLS), f32) * D_MODEL ** -0.5,
        "lb_param": nrm(ks[4], (DEPTH + 1, A_KEY_WIDTH), f32) * 0.5,
        "a_norm_w": gain(ks[5], (DEPTH, A_WIDTH)),
        "gmlp_ln_w": gain(ks[6], (DEPTH, B_WIDTH)),
        "gmlp_ln_b": 0.01 * nrm(ks[7], (DEPTH, B_WIDTH), f32),
        "w_spatial": nrm(ks[8], (DEPTH, B_GROUPS, GMLP_CHUNK, GMLP_CHUNK), f32) * GMLP_CHUNK ** -0.5,
        "b_spatial": gain(ks[9], (DEPTH, B_GROUPS, GMLP_CHUNK)),
        "w_out": nrm(ks[10], (DEPTH, MIX_WIDTH, D_MODEL), f32) * MIX_WIDTH ** -0.5,
        "post_mix_w": gain(ks[11], (DEPTH, D_MODEL)),
        "pre_ffn_w": gain(ks[12], (DEPTH, D_MODEL)),
        "w_gate": nrm(ks[13], (DEPTH, D_MODEL, D_FF), f32) * D_MODEL ** -0.5,
        "w_up": nrm(ks[14], (DEPTH, D_MODEL, D_FF), f32) * D_MODEL ** -0.5,
        "w_down": nrm(ks[15], (DEPTH, D_FF, D_MODEL), f32) * D_FF ** -0.5,
        "post_ffn_w": gain(ks[16], (DEPTH, D_MODEL)),
        "w_ple": nrm(ks[17], (DEPTH, PLE_DIM, D_MODEL), f32) * PLE_DIM ** -0.5,
        "w_ple_gate": nrm(ks[18], (DEPTH, D_MODEL, D_MODEL), f32) * D_MODEL ** -0.5,
        "post_ple_w": gain(ks[19], (DEPTH, D_MODEL)),
    }


def reference(x, p, pre_mix_w, w_in, lb_param, a_norm_w, gmlp_ln_w, gmlp_ln_b, w_spatial, b_spatial,
              w_out, post_mix_w, pre_ffn_w, w_gate, w_up, w_down, post_ffn_w, w_ple, w_ple_gate, post_ple_w):
    lower_bounds = jnp.cumsum(jax.nn.softmax(lb_param.astype(jnp.float32), axis=0), axis=0)
    splits = [A_KEY_WIDTH,
              2 * A_KEY_WIDTH,
              2 * A_KEY_WIDTH + A_WIDTH,
              2 * A_KEY_WIDTH + 2 * A_WIDTH,
              2 * A_KEY_WIDTH + 2 * A_WIDTH + B_WIDTH]
    for l in range(DEPTH):
        h = _rmsnorm(x, pre_mix_w[l])
        proj = h @ w_in[l]
        q, f_pre, i_in, g, u, v = jnp.split(proj, splits, axis=-1)
        a_out = _hgrn2(q, f_pre, i_in, g, lower_bounds[l], a_norm_w[l])
        b_out = _gmlp(u, v, gmlp_ln_w[l], gmlp_ln_b[l], w_spatial[l], b_spatial[l])
        mix = jnp.concatenate([a_out, b_out], axis=-1) @ w_out[l]
        x = x + _rmsnorm(mix, post_mix_w[l])
        h = _rmsnorm(x, pre_ffn_w[l])
        ff = (jax.nn.silu(h @ w_gate[l]) * (h @ w_up[l])) @ w_down[l]
        x = x + _rmsnorm(ff, post_ffn_w[l])
        gate = jax.nn.sigmoid(x @ w_ple_gate[l])
        x = x + _rmsnorm((p[l] @ w_ple[l]) * gate, post_ple_w[l])
    return x
```

```python
import numpy as np
import concourse.bass as bass
import concourse.mybir as mybir
from concourse.bass_utils import run_bass_kernel_spmd

F32, BF16 = mybir.dt.float32, mybir.dt.bfloat16
AF = mybir.ActivationFunctionType
ALU = mybir.AluOpType
AX = mybir.AxisListType

D = 4096
KC = 32
T = 512
TB = 4
DFF = 11008
NJ = 86
EPS = 1e-6
NSLAB = 86
ENGS = ("tensor", "vector", "scalar", "gpsimd", "sync")
OWN_DEPTH = 3
CACHE_W = True
WSTORE_Q = "gpsimd"
LATE_CACHE = ("g", "up", "pg")
DMA_ROT = {"dw": 8, "dl": 16, "ds": 16, "dx": 8}


class Tk:
    __slots__ = ("sem", "val")


class Buf:
    __slots__ = ("name", "w", "r")

    def __init__(self, name):
        self.name = name
        self.w = None
        self.r = {}


class Prog:
    def __init__(self):
        self.ops = {e: [] for e in ENGS}
        self.cnt = {e: 0 for e in ENGS}
        self.pend = {e: [] for e in ENGS}
        self.dcnt = {}
        self.dn = {}

    def op(self, eng, fn, reads=(), writes=(), signal=True, dma=None):
        deps = []
        for b in reads:
            if b.w is not None:
                deps.append(b.w)
        for b in writes:
            if b.w is not None:
                deps.append(b.w)
            deps.extend(b.r.values())
        t = Tk()
        if dma is not None:
            n = self.dn.get(dma, 0)
            self.dn[dma] = n + 1
            key = f"{dma}{n % DMA_ROT[dma]}"
            self.dcnt[key] = self.dcnt.get(key, 0) + 16
            t.sem = key
            t.val = self.dcnt[key]
            dma = key
        else:
            t.sem = eng
            if signal:
                self.cnt[eng] += 1
                t.val = self.cnt[eng]
                for p in self.pend[eng]:
                    p.val = t.val
                self.pend[eng] = []
            else:
                t.val = None
                self.pend[eng].append(t)
        for b in writes:
            b.w = t
            b.r = {}
        for b in reads:
            if b.w is not t:
                b.r[t.sem] = t
        self.ops[eng].append((fn, deps, t, signal, dma))
        return t

    def check(self):
        pos = {e: 0 for e in ENGS}
        val = {}
        progress = True
        while progress:
            progress = False
            for e in ENGS:
                while pos[e] < len(self.ops[e]):
                    fn, deps, t, signal, dma = self.ops[e][pos[e]]
                    ok = True
                    for d in deps:
                        assert d.val is not None, "unresolved ticket"
                        if d.sem == e and dma is None:
                            continue
                        if val.get(d.sem, 0) < d.val:
                            ok = False
                            break
                    if not ok:
                        break
                    if dma is not None:
                        val[dma] = val.get(dma, 0) + 16
                    elif signal:
                        val[e] = val.get(e, 0) + 1
                    pos[e] += 1
                    progress = True
        stuck = {e: (pos[e], len(self.ops[e])) for e in ENGS if pos[e] < len(self.ops[e])}
        assert not stuck, f"schedule deadlocks: {stuck}"

    def replay(self, eng, e, sems):
        seen = {}
        c = 0
        for fn, deps, t, signal, dma in self.ops[eng]:
            need = {}
            for d in deps:
                assert d.val is not None, "unresolved ticket"
                if d.sem == eng and dma is None:
                    if eng == "tensor":
                        continue
                    if d.val <= c - OWN_DEPTH:
                        continue
                if need.get(d.sem, 0) < d.val:
                    need[d.sem] = d.val
            for s, v in need.items():
                if seen.get(s, 0) < v:
                    e.wait_ge(sems[s], v)
                    seen[s] = v
            ins = fn(e)
            if dma is not None:
                ins.then_inc(sems[dma], 16)
            elif signal:
                ins.then_inc(sems[eng], 1)
                c += 1


def build(NT=2048, NP=2048, dbg=False):
    nc = bass.Bass("TRN2", target_bir_lowering=False)
    NTILES = NT // T
    PT = 1024 if NP >= 1024 else NP
    NPT = NP // PT if NP else 0

    def din(name, shape):
        return nc.dram_tensor(name, shape, F32, kind="ExternalInput").ap()

    x = din("x", [NT, D])
    xp = din("xp", [max(NP, 128), D])
    pin = din("p", [NT, 256])
    w_in = din("w_in", [D, 12288])
    w_out = din("w_out", [D, D])
    w_gate = din("w_gate", [D, DFF])
    w_up = din("w_up", [D, DFF])
    w_down = din("w_down", [DFF, D])
    w_ple = din("w_ple", [256, D])
    w_pg = din("w_ple_gate", [D, D])
    pre_mix_w = din("pre_mix_w", [32, 128])
    pre_ffn_w = din("pre_ffn_w", [32, 128])
    post_mix_w = din("post_mix_w", [1, D])
    post_ffn_w = din("post_ffn_w", [1, D])
    post_ple_w = din("post_ple_w", [1, D])
    lb_param = din("lb_param", [32, 128])
    a_norm_w = din("a_norm_w", [16, 128])
    ln_w = din("gmlp_ln_w", [1, 2048])
    ln_b = din("gmlp_ln_b", [1, 2048])
    w_sp = din("w_spatial", [16, 128, 128])
    b_sp = din("b_spatial", [1, 2048])
    y = nc.dram_tensor("y", [NT, D], F32, kind="ExternalOutput").ap()
    kind_dbg = dict(kind="ExternalOutput") if dbg else {}
    x1d = nc.dram_tensor("x1d", [NT, D], F32, **kind_dbg).ap()
    x2d = nc.dram_tensor("x2d", [NT, D], F32, **kind_dbg).ap()
    accd = nc.dram_tensor("accd", [2, T, D], F32, **kind_dbg).ap()
    rsd = nc.dram_tensor("rsd", [1, 2048], F32).ap()
    WSC_CH = 786432
    wscs = [nc.dram_tensor(f"wsc{i}", [128, WSC_CH], BF16, **kind_dbg).ap() for i in range(3)]
    if dbg:
        catd = nc.dram_tensor("catd", [NT // T, 128, 32, T], BF16, kind="ExternalOutput").ap()
        sd = nc.dram_tensor("sd", [128, 16, 128], F32, kind="ExternalOutput").ap()
        dbgp = nc.dram_tensor("dbgp", [NT // T, 128, 2, T], BF16, kind="ExternalOutput").ap()
        dbgw = nc.dram_tensor("dbgw", [NT // T, 128, 2, D], BF16, kind="ExternalOutput").ap()
        dbgb = nc.dram_tensor("dbgb", [NT // T, 128, 4, 256], BF16, kind="ExternalOutput").ap()

    P = Prog()
    from contextlib import ExitStack
    with ExitStack() as es:
        def sb(name, shape, dt):
            return es.enter_context(nc.sbuf_tensor(name, shape, dt))

        arena = sb("arena", [128, NSLAB * 512], BF16)
        hT = sb("hT", [128, KC, T], BF16)
        wsl = sb("wsl", [128, 8 * 4096], BF16)
        S = sb("S", [128, 16, 128], F32)
        WmT = sb("WmT", [128, 16, 128], BF16)
        cols = sb("cols", [128, 112], F32)
        lbc = sb("lbc", [128, 16], F32)
        oml = sb("oml", [128, 16], F32)
        ident = sb("ident", [128, 128], BF16)
        identf = sb("identf", [128, 128], F32)
        onesf = sb("onesf", [128, 128], F32)
        mask64 = sb("mask64", [64, 64], F32)
        ev = [sb(f"ev{i}", [128, 512], F32) for i in range(3)]
        ssacc = sb("ssacc", [128, TB, 8], F32)
        st = sb("st", [128, 64], F32)
        psum = [es.enter_context(nc.psum_tensor(f"ps{i}", [128, 512], F32)) for i in range(8)]
        sems = {e: es.enter_context(nc.semaphore("s_" + e)) for e in ENGS}
        for k, n in DMA_ROT.items():
            for i in range(n):
                sems[f"{k}{i}"] = es.enter_context(nc.semaphore(f"s_{k}{i}"))
        block = es.enter_context(nc.Block())

        AS = [Buf(f"a{j}") for j in range(NSLAB)]
        HBQ = [[Buf(f"h{j}_{tb}") for tb in range(TB)] for j in range(4)]
        HB = [b for q in HBQ for b in q]
        WB = [Buf(f"w{j}") for j in range(8)]
        PB = [[Buf(f"p{j}")] for j in range(8)]
        EV = [Buf(f"ev{j}") for j in range(3)]
        B_S = [Buf(f"S{h}") for h in range(16)]
        B_const = Buf("const")
        B_ss = Buf("ssacc")
        B_st = Buf("st")
        B_stn = [Buf("stn0"), Buf("stn1")]

        def av(s0, n, dt=BF16):
            ap = arena[:, s0 * 512:(s0 + n) * 512]
            if dt is F32:
                ap = ap.bitcast(F32)
            return ap

        def ab(s0, n):
            return AS[s0:s0 + n]

        def wv(s0, n, dt=BF16):
            ap = wsl[:, s0 * 4096:(s0 + n) * 4096]
            if dt is F32:
                ap = ap.bitcast(F32)
            return ap

        def pbf(i):
            return psum[i][:].bitcast(BF16)

        wpos = [0]
        wcap = [8]

        def walloc(n):
            a = 1 if n == 1 else (2 if n == 2 else 4)
            wpos[0] = (wpos[0] + a - 1) // a * a
            if wpos[0] + n > wcap[0]:
                wpos[0] = 0
            s0 = wpos[0]
            wpos[0] += n
            return s0

        wkeys = {}
        wuse = {}
        woff = [0]

        wpending = []

        def wflush():
            while wpending:
                flat, off, n, bufs, kb = wpending.pop(0)
                P.op(WSTORE_Q, lambda e, flat=flat, off=off, n=n: e.dma_start(out=wsc_ap(off, n), in_=flat),
                     reads=bufs, writes=[kb], dma="dx")

        def wsc_alloc(n):
            if woff[0] // WSC_CH != (woff[0] + n - 1) // WSC_CH:
                woff[0] = (woff[0] // WSC_CH + 1) * WSC_CH
            off = woff[0]
            woff[0] += n
            assert woff[0] <= 3 * WSC_CH
            return off

        def wsc_ap(off, n):
            return wscs[off // WSC_CH][:, off % WSC_CH:off % WSC_CH + n]

        def wload(key, src_ap, nk, ncols):
            n = nk * ncols
            nslots = (n + 4095) // 4096
            s0 = walloc(nslots)
            flat = wv(s0, nslots)[:, 0:n]
            dst = flat.rearrange("p (k n) -> p k n", n=ncols)
            bufs = WB[s0:s0 + nslots]
            if any(b in bufs for pend in wpending for b in pend[3]):
                wflush()
            if key not in wkeys:
                P.op("gpsimd", lambda e, dst=dst, src=src_ap: e.dma_start(out=dst, in_=src), writes=bufs, dma="dw")
                wflush()
                uses = wuse.get(key, 0)
                wuse[key] = uses + 1
                if key[0] not in LATE_CACHE or uses >= 1:
                    off = wsc_alloc(n)
                    kb = Buf("wk")
                    wkeys[key] = (off, kb)
                    wpending.append((flat, off, n, bufs, kb))
            else:
                off, kb = wkeys[key]
                P.op("gpsimd", lambda e, flat=flat, off=off, n=n: e.dma_start(out=flat, in_=wsc_ap(off, n)),
                     reads=[kb], writes=bufs, dma="dw")
                wflush()
            return dst, bufs

        evi = [0]

        def evnext():
            i = evi[0] % 3
            evi[0] += 1
            return i

        V = "vector"
        A = "scalar"
        G = "gpsimd"
        PE = "tensor"
        SY = "sync"

        P.op(G, lambda e: e.memset(identf[:], 1.0), writes=[B_const])
        P.op(G, lambda e: e.affine_select(identf[:], identf[:], [[-1, 128]], ALU.is_equal, 0.0, base=0,
                                          channel_multiplier=1), writes=[B_const])
        P.op(G, lambda e: e.tensor_copy(ident[:], identf[:]), writes=[B_const])
        P.op(G, lambda e: e.memset(onesf[:], 1.0), writes=[B_const])
        P.op(G, lambda e: e.memset(mask64[:], 1.0), writes=[B_const])
        P.op(G, lambda e: e.affine_select(mask64[:], mask64[:], [[1, 64]], ALU.is_ge, 0.0, base=0,
                                          channel_multiplier=-1), writes=[B_const])
        P.op(G, lambda e: e.memset(S[:], 0.0), writes=B_S)
        vrows = av(0, 1, F32)[0:112, 0:128]
        for r0, src, n in ((0, pre_mix_w, 32), (32, pre_ffn_w, 32), (64, lb_param, 32), (96, a_norm_w, 16)):
            P.op(SY, lambda e, r0=r0, src=src, n=n: e.dma_start(out=av(0, 1, F32)[r0:r0 + n, 0:128], in_=src),
                 writes=ab(0, 1), dma="dl")
        P.op(PE, lambda e: e.transpose(psum[0][:, 0:112], vrows, identf[0:112, 0:112]),
             reads=ab(0, 1) + [B_const], writes=PB[0])
        P.op(V, lambda e: e.tensor_copy(cols[:], psum[0][:, 0:112]), reads=PB[0], writes=[B_const])
        wcol_mix = cols[:, 0:32]
        wcol_ffn = cols[:, 32:64]
        anw = cols[:, 96:112]
        P.op(V, lambda e: e.tensor_tensor(st[:, 0:16], cols[:, 80:96], cols[:, 64:80], ALU.subtract),
             reads=[B_const], writes=[B_st])
        P.op(A, lambda e: e.activation(st[:, 16:32], st[:, 0:16], AF.Exp), reads=[B_st], writes=[B_st])
        P.op(V, lambda e: e.tensor_scalar_add(st[:, 32:48], st[:, 16:32], 1.0), reads=[B_st], writes=[B_st])
        P.op(V, lambda e: e.reciprocal(lbc[:], st[:, 32:48]), reads=[B_st], writes=[B_const])
        P.op(V, lambda e: e.tensor_tensor(oml[:], st[:, 16:32], lbc[:], ALU.mult), reads=[B_st, B_const],
             writes=[B_const])
        rscols = av(1, 1, F32)[:, 0:16]
        for g in range(16):
            wa = av(2 + (g % 2) * 2, 2, F32)[:, 0:128]
            wb_ = ab(2 + (g % 2) * 2, 2)
            P.op(SY, lambda e, wa=wa, g=g: e.dma_start(out=wa, in_=w_sp[g]), writes=wb_, dma="dl")
            P.op(G, lambda e, wa=wa: e.affine_select(wa, wa, [[-1, 128]], ALU.is_ge, 0.0, base=0,
                                                     channel_multiplier=1), reads=wb_, writes=wb_)
            P.op(V, lambda e, wa=wa, g=g: e.reduce_sum(rscols[:, g:g + 1], wa, AX.X), reads=wb_, writes=ab(1, 1))
            pb = 1 + g % 2
            P.op(PE, lambda e, wa=wa, pb=pb: e.transpose(psum[pb][:, 0:128], wa, identf[:]),
                 reads=wb_ + [B_const], writes=PB[pb])
            P.op(V, lambda e, g=g, pb=pb: e.tensor_copy(WmT[:, g, :], psum[pb][:, 0:128]), reads=PB[pb],
                 writes=[B_const])
        P.op(PE, lambda e: e.transpose(psum[3][0:16, 0:128], rscols, identf[:]), reads=ab(1, 1) + [B_const],
             writes=PB[3])
        rsrows = av(6, 1, F32)[0:16, 0:128]
        P.op(V, lambda e: e.tensor_copy(rsrows, psum[3][0:16, 0:128]), reads=PB[3], writes=ab(6, 1))
        B_rsd = Buf("rsd")
        P.op(SY, lambda e: e.dma_start(out=rsd.rearrange("o (g t) -> (o g) t", t=128), in_=rsrows), reads=ab(6, 1),
             writes=[B_rsd], dma="ds")

        def rstd_from_ss(ss_ap, n, out_ap, reads, extra_scale=1.0):
            P.op(A, lambda e: e.activation(out_ap, ss_ap, AF.Ln, scale=1.0 / n, bias=EPS), reads=[reads], writes=[reads])
            P.op(A, lambda e: e.activation(out_ap, out_ap, AF.Exp, scale=-0.5), reads=[reads], writes=[reads])

        def norm_pass(ntb, a_src, b_src, b_ss, wpost, dst, wcol, dstT, dstT_bufs, stage):
            for tb in range(ntb):
                a_ap, a_b = stage["a"][tb % 2]
                c0 = 52 + (tb % 2) * 4
                bst = B_stn[tb % 2]
                P.op(SY, lambda e, a_ap=a_ap, tb=tb: e.dma_start(out=a_ap, in_=a_src(tb)), reads=a_src.bufs(tb),
                     writes=a_b, dma="dl")
                if b_src is not None:
                    b_ap, b_b = stage["b"][tb % 2]
                    P.op(SY, lambda e, b_ap=b_ap, tb=tb: e.dma_start(out=b_ap, in_=b_src(tb)), reads=b_src.bufs(tb),
                         writes=b_b, dma="dl")
                    P.op(V, lambda e, tb=tb, c0=c0: e.reduce_sum(st[:, c0:c0 + 1], ssacc[:, tb, :], AX.X), reads=[B_ss],
                         writes=[bst])
                    rstd_from_ss(st[:, c0:c0 + 1], D, st[:, c0 + 1:c0 + 2], bst)
                    P.op(V, lambda e, a_ap=a_ap, b_ap=b_ap, c0=c0: e.scalar_tensor_tensor(
                        a_ap, b_ap, st[:, c0 + 1:c0 + 2], a_ap, ALU.mult, ALU.add),
                         reads=a_b + b_b + [bst], writes=a_b)
                if dst is not None:
                    P.op(SY, lambda e, a_ap=a_ap, tb=tb: e.dma_start(out=dst(tb), in_=a_ap), reads=a_b,
                         writes=dst.bufs(tb), dma="ds")
                if dstT is None:
                    continue
                xn_ap, xn_b = stage["xn"]
                if wcol is not None:
                    P.op(A, lambda e, a_ap=a_ap, c0=c0: e.activation(xn_ap, a_ap, AF.Square,
                                                                     accum_out=st[:, c0 + 2:c0 + 3]),
                         reads=a_b, writes=xn_b + [bst])
                    rstd_from_ss(st[:, c0 + 2:c0 + 3], D, st[:, c0 + 3:c0 + 4], bst)
                    P.op(A, lambda e, a_ap=a_ap, c0=c0: e.activation(xn_ap, a_ap, AF.Copy, scale=st[:, c0 + 3:c0 + 4]),
                         reads=a_b + [bst], writes=xn_b)
                else:
                    P.op(A, lambda e, a_ap=a_ap: e.activation(xn_ap, a_ap, AF.Copy), reads=a_b, writes=xn_b)
                for q in range(4):
                    pb = (tb % 2) * 4 + q
                    def tr(e, q=q, pb=pb):
                        for k in range(8):
                            kc = q * 8 + k
                            ins = e.transpose(pbf(pb)[:, k * 128:(k + 1) * 128], xn_ap[:, kc * 128:(kc + 1) * 128],
                                              ident[:])
                        return ins
                    P.op(PE, tr, reads=xn_b + [B_const], writes=PB[pb])
                    src3 = pbf(pb).rearrange("p (k t) -> p k t", t=128)
                    dst3 = dstT(q, tb)
                    if wcol is not None:
                        wc = wcol[:, q * 8:(q + 1) * 8].unsqueeze(2).to_broadcast([128, 8, 128])
                        P.op(V, lambda e, src3=src3, dst3=dst3, wc=wc: e.tensor_tensor(dst3, src3, wc, ALU.mult),
                             reads=(PB[pb] + [B_const]), writes=dstT_bufs(q, tb))
                    else:
                        P.op(V, lambda e, src3=src3, dst3=dst3: e.tensor_copy(dst3, src3), reads=PB[pb],
                             writes=dstT_bufs(q, tb))

        class Rows:
            def __init__(self, ap, row0, name):
                self.ap = ap
                self.row0 = row0
                self.b = {}
                self.name = name

            def __call__(self, tb):
                r = self.row0 + tb * 128
                return self.ap[r:r + 128, :]

            def bufs(self, tb):
                return [self.b.setdefault(tb, Buf(f"{self.name}{tb}"))]

        class Head:
            def __init__(self, h, proj, ntok, ch, full, fe0, cl0, cat_slab):
                self.h, self.proj, self.ntok, self.ch, self.full = h, proj, ntok, ch, full
                self.nch = ntok // ch
                self.fe_cur = [fe0]
                self.cl_cur = [cl0]
                self.cat_slab = cat_slab

            def _t(self, cur, n, dt):
                a0 = cur[0]
                cur[0] += n
                assert cur[0] <= NSLAB
                return av(a0, n, dt), ab(a0, n)

            def uloc(self, c):
                if self.nch == 8:
                    return 6 + c // 4, (c % 4) * 128, PB[6 + c // 4]
                return 7, c * 128, PB[7]

            def fe_elem(self):
                h, proj, ntok, ch, nch, full = self.h, self.proj, self.ntok, self.ch, self.nch, self.full
                fe = lambda n=2, dt=F32: self._t(self.fe_cur, n, dt)
                cl = lambda n=2, dt=F32: self._t(self.cl_cur, n, dt)
                fb, ib = proj["f"], proj["i"]
                ef, ef_b = fe()
                ef = ef[:, 0:ntok]
                P.op(A, lambda e: e.activation(ef, psum[fb][:, 0:ntok], AF.Sigmoid, scale=-1.0), reads=PB[fb],
                     writes=ef_b)
                iTb, iTb_b = fe(1, BF16)
                iTb = iTb[:, 0:ntok]
                P.op(A, lambda e: e.activation(iTb, psum[ib][:, 0:ntok], AF.Copy), reads=PB[ib], writes=iTb_b)
                if full:
                    qb, gb = proj["q"], proj["g"]
                    eq_, eq_b = fe()
                    eq_ = eq_[:, 0:ntok]
                    P.op(A, lambda e: e.activation(eq_, psum[qb][:, 0:ntok], AF.Silu), reads=PB[qb], writes=eq_b)
                    eg, eg_b = cl()
                    eg = eg[:, 0:ntok]
                    qs, qs_b, sg, sg_b = eq_, eq_b, eg, eg_b
                    self.sg, self.sg_b = sg, sg_b
                self._loc = dict(locals())

            def fe_b(self):
                L = self._loc
                h, proj, ntok, ch, nch, full = self.h, self.proj, self.ntok, self.ch, self.nch, self.full
                fe, cl, ef, ef_b, iTb, iTb_b = L["fe"], L["cl"], L["ef"], L["ef_b"], L["iTb"], L["iTb_b"]
                if full:
                    gb, eg, eg_b, qs, qs_b = L["gb"], L["eg"], L["eg_b"], L["qs"], L["qs_b"]
                    P.op(A, lambda e: e.activation(eg, psum[gb][:, 0:ntok], AF.Silu), reads=PB[gb], writes=eg_b)
                rf, rf_b = fe()
                rf = rf[:, 0:ntok]
                kT, kT_b = fe()
                kT = kT[:, 0:ntok]
                P.op(V, lambda e: e.tensor_scalar(kT, ef, oml[:, h:h + 1], None, ALU.mult), reads=ef_b + [B_const],
                     writes=kT_b)
                lf = ef
                P.op(A, lambda e: e.activation(lf, kT, AF.Ln, scale=-1.0, bias=1.0), reads=kT_b, writes=ef_b)
                Bp, Bp_b = fe(3)
                Bp = Bp[:, 0:ntok + 1]
                P.op(V, lambda e: e.memset(Bp[:, 0:1], 0.0), writes=Bp_b)
                P.op(V, lambda e: e.tensor_tensor_scan(Bp[:, 1:ntok + 1], onesf[:, 0:1].to_broadcast([128, ntok]), lf,
                                                       0.0, ALU.mult, ALU.add), reads=ef_b + [B_const], writes=Bp_b)
                brel, brel_b = rf, rf_b
                b3 = brel.rearrange("p (c t) -> p c t", t=ch)
                P.op(V, lambda e: e.tensor_tensor(b3, Bp[:, 1:ntok + 1].rearrange("p (c t) -> p c t", t=ch),
                                                  Bp[:, 0:ntok].rearrange("p (c t) -> p c t", t=ch)[:, :, 0:1]
                                                  .to_broadcast([128, nch, ch]), ALU.subtract),
                     reads=Bp_b, writes=brel_b)
                d2, d2_b = fe()
                d2 = d2[:, 0:ntok]
                d23 = d2.rearrange("p (c t) -> p c t", t=ch)
                P.op(V, lambda e: e.tensor_tensor(d23, b3[:, :, ch - 1:ch].to_broadcast([128, nch, ch]), b3,
                                                  ALU.subtract), reads=brel_b, writes=d2_b)
                P.op(A, lambda e: e.activation(d2, d2, AF.Exp), reads=d2_b, writes=d2_b)
                edec, edec_b = cl(1)
                edec = edec[:, 0:nch]
                P.op(A, lambda e: e.activation(edec.unsqueeze(2), b3[:, :, ch - 1:ch], AF.Exp), reads=brel_b,
                     writes=edec_b)
                self.edec, self.edec_b = edec, edec_b
                khT, khT_b = fe(1, BF16)
                khT = khT[:, 0:ntok]
                P.op(V, lambda e: e.tensor_tensor(khT, kT, d2, ALU.mult), reads=kT_b + d2_b, writes=khT_b)
                if full:
                    ex, ex_b = fe()
                    ex = ex[:, 0:ntok]
                    P.op(A, lambda e: e.activation(ex, brel, AF.Exp), reads=brel_b, writes=ex_b)
                    qt, qt_b = cl(1, BF16)
                    qt = qt[:, 0:ntok]
                    P.op(V, lambda e: e.tensor_tensor(qt, qs, ex, ALU.mult), reads=qs_b + ex_b, writes=qt_b)
                    ex2, ex2_b = fe()
                    ex2 = ex2[:, 0:ntok]
                    P.op(A, lambda e: e.activation(ex2, brel, AF.Exp, scale=-1.0), reads=brel_b, writes=ex2_b)
                    kt, kt_b = fe(1, BF16)
                    kt = kt[:, 0:ntok]
                    P.op(V, lambda e: e.tensor_tensor(kt, kT, ex2, ALU.mult), reads=kT_b + ex2_b, writes=kt_b)
                    self.qt, self.qt_b, self.kt, self.kt_b = qt, qt_b, kt, kt_b
                    sT, sT_b = cl(1, BF16)
                    self.sTm = [sT[0:ch, i * ch:(i + 1) * ch] for i in range(nch)]
                    self.sTm_b = sT_b
                    s0_, s0_b = cl(1, BF16)
                    s1_, s1_b = cl(1, BF16)
                    self.Sbf = [(s0_[:, 0:128], s0_b), (s1_[:, 0:128], s1_b)]
                    self.osq, self.osq_b = cl()
                    self.rs, self.rs_b = cl()
                nsl = (nch * 128 * 2 + 1023) // 1024
                vt, self.vtok_b = cl(nsl, BF16)
                self.vtok = vt[0:ch, 0:nch * 128].rearrange("p (c v) -> p c v", v=128)
                ktk, self.ktok_b = fe(nsl, BF16)
                self.ktok = ktk[0:ch, 0:nch * 128].rearrange("p (c v) -> p c v", v=128)
                self.iTb, self.iTb_b, self.khT, self.khT_b = iTb, iTb_b, khT, khT_b

            def fe_pe(self):
                h, ntok, ch, nch, full = self.h, self.ntok, self.ch, self.nch, self.full
                pbv, pbk = (6, 7) if full else (5, 6)
                for (srcT, srcT_b, dtok, dtok_b, pb) in ((self.iTb, self.iTb_b, self.vtok, self.vtok_b, pbv),
                                                        (self.khT, self.khT_b, self.ktok, self.ktok_b, pbk)):
                    def tr(e, srcT=srcT, pb=pb):
                        for c in range(nch):
                            ins = e.transpose(pbf(pb)[0:ch, c * 128:(c + 1) * 128], srcT[:, c * ch:(c + 1) * ch],
                                              ident[:])
                        return ins
                    P.op(PE, tr, reads=srcT_b + [B_const], writes=PB[pb])
                    P.op(V, lambda e, dtok=dtok, pb=pb: e.tensor_copy(
                        dtok, pbf(pb)[0:ch, 0:nch * 128].rearrange("p (c v) -> p c v", v=128)), reads=PB[pb],
                         writes=dtok_b)
                if full:
                    P.op(A, lambda e: e.activation(self.Sbf[0][0], S[:, h, :], AF.Copy), reads=[B_S[h]],
                         writes=self.Sbf[0][1])
                if full:
                    for c in range(nch):
                        cs = slice(c * ch, (c + 1) * ch)
                        sps = psum[5][0:ch, c * ch:(c + 1) * ch]
                        P.op(PE, lambda e, cs=cs, sps=sps: e.matmul(sps, self.kt[:, cs], self.qt[:, cs], start=True,
                                                                    stop=True),
                             reads=self.kt_b + self.qt_b, writes=PB[5])
                order = list(range(nch))
                if nch == 8:
                    order = [0, 1, 2, 3, 4, 5, 6, 7]
                for c in order:
                    ubk, ucol, ub_ = self.uloc(c)
                    P.op(PE, lambda e, c=c, ubk=ubk, ucol=ucol: e.matmul(psum[ubk][:, ucol:ucol + 128],
                                                                         self.ktok[:, c, :], self.vtok[:, c, :],
                                                                         start=True, stop=True),
                         reads=self.ktok_b + self.vtok_b, writes=ub_)
                if full:
                    for c in range(nch):
                        sps = psum[5][0:ch, c * ch:(c + 1) * ch]
                        P.op(V, lambda e, sps=sps, c=c: e.tensor_tensor(self.sTm[c], sps, mask64[0:ch, 0:ch],
                                                                        ALU.mult),
                             reads=PB[5] + [B_const], writes=self.sTm_b)

            def chain_step(self, c):
                h, ch, nch, full = self.h, self.ch, self.nch, self.full
                cs = slice(c * ch, (c + 1) * ch)
                if full:
                    sb_ap, sb_b = self.Sbf[c % 2]

                    def om(e, c=c, cs=cs, sb_ap=sb_ap):
                        e.matmul(psum[4][:, cs], self.vtok[:, c, :], self.sTm[c], start=True, stop=False)
                        return e.matmul(psum[4][:, cs], sb_ap, self.qt[:, cs], start=False, stop=True)
                    P.op(PE, om, reads=self.vtok_b + self.sTm_b + sb_b + self.qt_b, writes=PB[4])
                ubk, ucol, ub_ = self.uloc(c)
                P.op(V, lambda e, c=c, ubk=ubk, ucol=ucol: e.scalar_tensor_tensor(
                    S[:, h, :], S[:, h, :], self.edec[:, c:c + 1], psum[ubk][:, ucol:ucol + 128], ALU.mult, ALU.add),
                     reads=ub_ + [B_S[h]] + self.edec_b, writes=[B_S[h]])
                if full and c < nch - 1:
                    nb_ap, nb_b = self.Sbf[(c + 1) % 2]
                    P.op(A, lambda e, nb_ap=nb_ap: e.activation(nb_ap, S[:, h, :], AF.Copy), reads=[B_S[h]],
                         writes=nb_b)

            def tail_a(self):
                if not self.full:
                    return
                ntok = self.ntok
                osq, osq_b = self.osq[:, 0:ntok], self.osq_b
                P.op(A, lambda e: e.activation(osq, psum[4][:, 0:ntok], AF.Square), reads=PB[4], writes=osq_b)

            def tail_b(self):
                if not self.full:
                    return
                h, ntok = self.h, self.ntok
                osq, osq_b, rs_, rs_b = self.osq[:, 0:ntok], self.osq_b, self.rs[:, 0:ntok], self.rs_b
                P.op(PE, lambda e: e.matmul(psum[5][:, 0:ntok], onesf[:], osq, start=True, stop=True),
                     reads=osq_b + [B_const], writes=PB[5])
                P.op(A, lambda e: e.activation(rs_, psum[5][:, 0:ntok], AF.Ln, scale=1.0 / 128, bias=EPS),
                     reads=PB[5], writes=rs_b)
                P.op(A, lambda e: e.activation(rs_, rs_, AF.Exp, scale=-0.5), reads=rs_b, writes=rs_b)
                P.op(V, lambda e: e.tensor_tensor(rs_, psum[4][:, 0:ntok], rs_, ALU.mult), reads=rs_b + PB[4],
                     writes=rs_b)
                cat = av(self.cat_slab, 1)[:, 0:ntok]
                P.op(V, lambda e: e.scalar_tensor_tensor(cat, rs_, anw[:, h:h + 1], self.sg, ALU.mult, ALU.mult),
                     reads=rs_b + self.sg_b + [B_const], writes=ab(self.cat_slab, 1))

        wsrc_in = w_in.rearrange("(k p) n -> p k n", p=128)
        for ph in range(NPT):
            nslab_h = KC * PT // 512
            hp3 = av(0, nslab_h).rearrange("p (k t) -> p k t", t=PT)
            ntb = PT // 128
            stage = {"a": [(wv(0, 2, F32), WB[0:2]), (wv(2, 2, F32), WB[2:4])], "xn": (wv(4, 1), WB[4:5])}
            src = Rows(xp, ph * PT, "xp")
            wflush()
            gsz = 8 * PT // 512

            def dstT(q, tb, hp3=hp3):
                return hp3[:, q * 8:(q + 1) * 8, tb * 128:(tb + 1) * 128]

            def dstT_bufs(q, tb, gsz=gsz):
                return ab(q * gsz, gsz)
            norm_pass(ntb, src, None, None, None, None, wcol_mix, dstT, dstT_bufs, stage)
            ntg = PT // 512
            t0 = nslab_h
            units = [(h, tg) for h in range(16) for tg in range(ntg)]
            pw = {}

            def emit_proj(u, hp3=hp3):
                h, tg = units[u]
                if h % 2 == 0 and tg == 0:
                    hp_ = h // 2
                    pw["f"] = wload(("pf", hp_), wsrc_in[:, :, 2048 + hp_ * 256:2048 + (hp_ + 1) * 256], KC, 256)
                    pw["i"] = wload(("pi", hp_), wsrc_in[:, :, 4096 + hp_ * 256:4096 + (hp_ + 1) * 256], KC, 256)
                hh = h % 2
                for nm, bank in (("f", (u % 2) * 2), ("i", (u % 2) * 2 + 1)):
                    wt, wb_ = pw[nm]

                    def mm(e, wt=wt, hh=hh, tg=tg, bank=bank):
                        for kc in range(KC):
                            ins = e.matmul(psum[bank][:, :], wt[:, kc, hh * 128:(hh + 1) * 128],
                                           hp3[:, kc, tg * 512:(tg + 1) * 512], start=(kc == 0), stop=(kc == KC - 1))
                        return ins
                    P.op(PE, mm, reads=wb_ + ab(0, nslab_h), writes=PB[bank])

            def mk_head(u):
                h, tg = units[u]
                return Head(h, {"f": (u % 2) * 2, "i": (u % 2) * 2 + 1}, 512, 128, False, t0, t0 + 14 + (u % 2) * 2, None)
            emit_proj(0)
            H = mk_head(0)
            H.fe_elem()
            H.fe_b()
            for u in range(len(units)):
                if u + 1 < len(units):
                    emit_proj(u + 1)
                H.fe_pe()
                for c in range(4):
                    H.chain_step(c)
                if u + 1 < len(units):
                    H = mk_head(u + 1)
                    H.fe_elem()
                    H.fe_b()

        wsrc_out = w_out.rearrange("(k p) n -> p k n", p=128)
        wsrc_g = w_gate.rearrange("(k p) n -> p k n", p=128)
        wsrc_u = w_up.rearrange("(k p) n -> p k n", p=128)
        wsrc_d = w_down.rearrange("(j p) n -> p j n", p=128)
        wsrc_pg = w_pg.rearrange("(k p) n -> p k n", p=128)
        wsrc_ple = w_ple.rearrange("(k p) n -> p k n", p=128)
        h3 = hT[:]

        def hT_dst(q, tb):
            return h3[:, q * 8:(q + 1) * 8, tb * 128:(tb + 1) * 128]

        def hT_bufs(q, tb):
            return [HBQ[q][tb]]

        def hT_tb(tb):
            return [HBQ[q][tb] for q in range(4)]

        def acc_evac(bank, tb, cb, accrows, wp):
            wp_ap, wp_b = wp
            i = evnext()
            P.op(A, lambda e: e.activation(ev[i][:], psum[bank][:], AF.Square, accum_out=ssacc[:, tb, cb:cb + 1]),
                 reads=PB[bank], writes=[EV[i], B_ss])
            P.op(V, lambda e: e.tensor_tensor(ev[i][:], psum[bank][:], wp_ap[:, cb * 512:(cb + 1) * 512], ALU.mult),
                 reads=PB[bank] + wp_b, writes=[EV[i]])
            P.op(SY, lambda e: e.dma_start(out=accrows(tb)[:, cb * 512:(cb + 1) * 512], in_=ev[i][:]), reads=[EV[i]],
                 writes=accrows.bufs(tb), dma="ds")

        def load_wp(vec, ap, bufs):
            wflush()
            P.op(SY, lambda e: e.dma_start(out=ap, in_=vec.partition_broadcast(128)), writes=bufs, dma="dl")
            return ap, bufs

        def std_stage():
            return {"a": [(av(0, 16, F32), ab(0, 16)), (av(16, 16, F32), ab(16, 16))],
                    "b": [(av(32, 16, F32), ab(32, 16)), (av(48, 16, F32), ab(48, 16))],
                    "xn": (av(64, 8), ab(64, 8))}

        for ti in range(NTILES):
            r0 = ti * T
            xrows = Rows(x, r0, f"x{ti}_")
            x1rows = Rows(x1d, r0, f"x1_{ti}_")
            x2rows = Rows(x2d, r0, f"x2_{ti}_")
            yrows = Rows(y, r0, f"y{ti}_")
            accA = Rows(accd[ti % 2], 0, f"acc{ti}_")
            norm_pass(TB, xrows, None, None, None, None, wcol_mix, hT_dst, hT_bufs, std_stage())
            GV0 = 32
            gv = av(GV0, 16).rearrange("p (b n) -> p b n", n=2048)
            lnw_bc = av(48, 8, F32)
            P.op(SY, lambda e: e.dma_start(out=lnw_bc, in_=ln_w.partition_broadcast(128)), writes=ab(48, 8), dma="dl")
            l2 = av(56, 8, F32)[0:2, :]
            r2 = av(64, 8, F32)[0:2, :]
            P.op(V, lambda e: e.memset(l2, 1.0), writes=ab(56, 8))
            P.op(SY, lambda e: e.dma_start(out=av(56, 8, F32)[0:1, :], in_=ln_b), writes=ab(56, 8), dma="dl")
            P.op(SY, lambda e: e.dma_start(out=av(64, 8, F32)[0:1, :], in_=rsd), reads=[B_rsd], writes=ab(64, 8),
                 dma="dl")
            P.op(SY, lambda e: e.dma_start(out=av(64, 8, F32)[1:2, :], in_=b_sp), writes=ab(64, 8), dma="dl")
            for cg in range(4):
                wt, wb_ = wload(("v", cg), wsrc_in[:, :, 10240 + cg * 512:10240 + (cg + 1) * 512], KC, 512)
                for tb in range(TB):
                    bank = (cg * TB + tb) % 4

                    def mm(e, wt=wt, tb=tb, bank=bank):
                        for kc in range(KC):
                            ins = e.matmul(psum[bank][:], h3[:, kc, tb * 128:(tb + 1) * 128], wt[:, kc, :],
                                           start=(kc == 0), stop=(kc == KC - 1))
                        return ins
                    P.op(PE, mm, reads=wb_ + hT_tb(tb), writes=PB[bank])
                    gslab = ab(GV0 + tb * 4 + cg, 1)
                    P.op(A, lambda e, tb=tb, cg=cg, bank=bank: e.activation(
                        gv[:, tb, cg * 512:(cg + 1) * 512], psum[bank][:], AF.Gelu,
                        accum_out=st[:, 8 + tb * 4 + cg:9 + tb * 4 + cg]), reads=PB[bank], writes=gslab + [B_st])
                    i = evnext()
                    P.op(A, lambda e, tb=tb, cg=cg, i=i: e.activation(
                        ev[i][:], gv[:, tb, cg * 512:(cg + 1) * 512], AF.Square,
                        accum_out=st[:, 24 + tb * 4 + cg:25 + tb * 4 + cg]), reads=gslab, writes=[EV[i], B_st])
            s1 = st[:, 8:24].rearrange("p (b c) -> p b c", c=4)
            s2 = st[:, 24:40].rearrange("p (b c) -> p b c", c=4)
            P.op(V, lambda e: e.reduce_sum(st[:, 40:44], s1, AX.X), reads=[B_st], writes=[B_st])
            P.op(V, lambda e: e.reduce_sum(st[:, 44:48], s2, AX.X), reads=[B_st], writes=[B_st])
            P.op(V, lambda e: e.tensor_scalar_mul(st[:, 40:44], st[:, 40:44], 1.0 / 2048), reads=[B_st], writes=[B_st])
            P.op(V, lambda e: e.tensor_tensor(st[:, 48:52], st[:, 40:44], st[:, 40:44], ALU.mult), reads=[B_st],
                 writes=[B_st])
            P.op(V, lambda e: e.scalar_tensor_tensor(st[:, 44:48], st[:, 44:48], 1.0 / 2048, st[:, 48:52], ALU.mult,
                                                     ALU.subtract), reads=[B_st], writes=[B_st])
            P.op(A, lambda e: e.activation(st[:, 44:48], st[:, 44:48], AF.Ln, bias=EPS), reads=[B_st], writes=[B_st])
            P.op(A, lambda e: e.activation(st[:, 44:48], st[:, 44:48], AF.Exp, scale=-0.5), reads=[B_st],
                 writes=[B_st])
            P.op(V, lambda e: e.scalar_tensor_tensor(st[:, 48:52], st[:, 40:44], -1.0, st[:, 44:48], ALU.mult,
                                                     ALU.mult), reads=[B_st], writes=[B_st])
            for tb in range(TB):
                gs = ab(GV0 + tb * 4, 4)
                P.op(V, lambda e, tb=tb: e.tensor_scalar(gv[:, tb, :], gv[:, tb, :], st[:, 44 + tb:45 + tb],
                                                         st[:, 48 + tb:49 + tb], ALU.mult, ALU.add),
                     reads=gs + [B_st], writes=gs)
                P.op(V, lambda e, tb=tb: e.tensor_tensor(gv[:, tb, :], gv[:, tb, :], lnw_bc, ALU.mult),
                     reads=gs + ab(48, 8), writes=gs)
            for uq in range(4):
                wt, wb_ = wload(("u", uq), wsrc_in[:, :, 8192 + uq * 512:8192 + (uq + 1) * 512], KC, 512)
                for gg in range(4):
                    g = uq * 4 + gg
                    ub = 4 + g % 2
                    zb = 6 + g % 2

                    def mm(e, wt=wt, gg=gg, ub=ub):
                        for kc in range(KC):
                            ins = e.matmul(psum[ub][:], wt[:, kc, gg * 128:(gg + 1) * 128], h3[:, kc, :],
                                           start=(kc == 0), stop=(kc == KC - 1))
                        return ins
                    P.op(PE, mm, reads=wb_ + HB, writes=PB[ub])
                    gu = av(72 + g % 2, 1)
                    gu_b = ab(72 + g % 2, 1)
                    P.op(A, lambda e, gu=gu, ub=ub: e.activation(gu, psum[ub][:], AF.Gelu), reads=PB[ub],
                         writes=gu_b)

                    def zm(e, g=g, zb=zb):
                        for tb in range(TB):
                            e.matmul(psum[zb][:, tb * 128:(tb + 1) * 128], gv[:, tb, g * 128:(g + 1) * 128],
                                     WmT[:, g, :], start=True, stop=False)
                            ins = e.matmul(psum[zb][:, tb * 128:(tb + 1) * 128], l2[:, g * 128:(g + 1) * 128],
                                           r2[:, g * 128:(g + 1) * 128], start=False, stop=True)
                        return ins
                    P.op(PE, zm, reads=ab(GV0, 16) + ab(56, 16) + [B_const], writes=PB[zb])
                    P.op(V, lambda e, gu=gu, zb=zb, g=g: e.tensor_tensor(av(16 + g, 1), psum[zb][:], gu, ALU.mult),
                         reads=PB[zb] + gu_b, writes=ab(16 + g, 1))
            banks = {"q": 0, "f": 1, "i": 2, "g": 3}

            def load_head_w(h):
                return {nm: wload(("hd", h, nm), wsrc_in[:, :, c0 + h * 128:c0 + (h + 1) * 128], KC, 128)
                        for nm, c0 in (("f", 2048), ("i", 4096), ("q", 0), ("g", 6144))}

            def proj_piece(wt, wb_, bank, k0, k1):
                def mm(e):
                    for kc in range(k0, k1):
                        ins = e.matmul(psum[bank][:], wt[:, kc, :], h3[:, kc, :], start=(kc == 0), stop=(kc == KC - 1))
                    return ins
                P.op(PE, mm, reads=wb_ + HB, writes=PB[bank], signal=(k1 == KC))
            wts = load_head_w(0)
            for nm in ("f", "i", "q", "g"):
                proj_piece(*wts[nm], banks[nm], 0, KC)
            H = Head(0, banks, T, 64, True, 32, 54, 0)
            H.fe_elem()
            H.fe_b()
            Hprev = None
            for h in range(16):
                nxt = h + 1 < 16
                if nxt:
                    wts = load_head_w(h + 1)
                    proj_piece(*wts["f"], banks["f"], 0, KC)
                    proj_piece(*wts["i"], banks["i"], 0, KC)
                if Hprev is not None:
                    Hprev.tail_b()
                if nxt:
                    proj_piece(*wts["q"], banks["q"], 0, KC)
                H.fe_pe()
                if nxt:
                    Hn = Head(h + 1, banks, T, 64, True, 32, 54 + ((h + 1) % 2) * 16, h + 1)
                    Hn.fe_elem()
                for c in range(8):
                    if nxt:
                        proj_piece(*wts["g"], banks["g"], c * 4, c * 4 + 4)
                    H.chain_step(c)
                H.tail_a()
                if nxt:
                    Hn.fe_b()
                Hprev = H
                if nxt:
                    H = Hn
            Hprev.tail_b()
            if dbg:
                P.op(SY, lambda e, ti=ti: e.dma_start(out=catd[ti], in_=av(0, 32).rearrange("p (k t) -> p k t", t=T)),
                     reads=ab(0, 32), writes=[Buf("x")], dma="ds")
            cat3 = av(0, 32).rearrange("p (k t) -> p k t", t=T)

            def mm_A(src3, src_bufs, nk, wsrc, accrows, wp, wname, post=None):
                for cb in range(8):
                    wt, wb_ = wload((wname, cb), wsrc[:, :, cb * 512:(cb + 1) * 512], nk, 512)
                    for tb in range(TB):
                        bank = (cb % 2) * 4 + tb

                        def mm(e, wt=wt, tb=tb, bank=bank):
                            for kc in range(nk):
                                ins = e.matmul(psum[bank][:], src3[:, kc, tb * 128:(tb + 1) * 128], wt[:, kc, :],
                                               start=(kc == 0), stop=(kc == nk - 1))
                            return ins
                        P.op(PE, mm, reads=wb_ + src_bufs, writes=PB[bank])
                        if post is None:
                            acc_evac(bank, tb, cb, accrows, wp)
                        else:
                            post(bank, tb, cb)
            wp2 = load_wp(post_mix_w, av(32, 16, F32), ab(32, 16))
            mm_A(cat3, ab(0, 32), KC, wsrc_out, accA, wp2, "out")
            norm_pass(TB, xrows, accA, None, post_mix_w, x1rows, wcol_ffn, hT_dst, hT_bufs, std_stage())
            for jq in range(43):
                wtg, wg_b = wload(("g", jq), wsrc_g[:, :, jq * 256:(jq + 1) * 256], KC, 256)
                wtu, wu_b = wload(("up", jq), wsrc_u[:, :, jq * 256:(jq + 1) * 256], KC, 256)
                for jj in range(2):
                    j = jq * 2 + jj
                    gbk = (j % 2) * 2
                    ubk = gbk + 1
                    for wt, wb_, bank in ((wtg, wg_b, gbk), (wtu, wu_b, ubk)):
                        def mm(e, wt=wt, jj=jj, bank=bank):
                            for kc in range(KC):
                                ins = e.matmul(psum[bank][:], wt[:, kc, jj * 128:(jj + 1) * 128], h3[:, kc, :],
                                               start=(kc == 0), stop=(kc == KC - 1))
                            return ins
                        P.op(PE, mm, reads=wb_ + HB, writes=PB[bank])
                    i = evnext()
                    P.op(A, lambda e, i=i, gbk=gbk: e.activation(ev[i][:], psum[gbk][:], AF.Silu), reads=PB[gbk],
                         writes=[EV[i]])
                    P.op(V, lambda e, i=i, ubk=ubk, j=j: e.tensor_tensor(av(j, 1), ev[i][:], psum[ubk][:], ALU.mult),
                         reads=([EV[i]] + PB[ubk]), writes=ab(j, 1))
            accB = Rows(accd[ti % 2], 0, f"accb{ti}_")
            accB.b = accA.b
            act3 = av(0, NJ).rearrange("p (j t) -> p j t", t=T)
            wcap[0] = 6
            if wpos[0] > 6:
                wpos[0] = 0
            wp3 = load_wp(post_ffn_w, wv(6, 2, F32), WB[6:8])
            for cb in range(8):
                for jg in range(6):
                    nj = 16 if jg < 5 else 6
                    wt, wb_ = wload(("d", cb, jg), wsrc_d[:, jg * 16:jg * 16 + nj, cb * 512:(cb + 1) * 512], nj, 512)
                    for tb in range(TB):
                        bank = (cb % 2) * 4 + tb

                        def mm(e, wt=wt, tb=tb, bank=bank, jg=jg, nj=nj):
                            for jj in range(nj):
                                j = jg * 16 + jj
                                ins = e.matmul(psum[bank][:], act3[:, j, tb * 128:(tb + 1) * 128], wt[:, jj, :],
                                               start=(j == 0), stop=(j == NJ - 1))
                            return ins
                        P.op(PE, mm, reads=wb_ + ab(jg * 16, nj), writes=PB[bank], signal=True)
                        if jg == 5:
                            acc_evac(bank, tb, cb, accB, wp3)
            wcap[0] = 8
            norm_pass(TB, x1rows, accB, None, post_ffn_w, x2rows, None, hT_dst, hT_bufs, std_stage())
            accC = Rows(accd[ti % 2], 0, f"accc{ti}_")
            accC.b = accA.b
            pst = av(0, 4, F32).rearrange("p (b n) -> p b n", n=256)
            P.op(SY, lambda e, r0=r0: e.dma_start(out=pst, in_=pin[r0:r0 + T, :].rearrange("(b p) n -> p b n", p=128)),
                 writes=ab(0, 4), dma="dl")
            pbf_ = av(4, 2).rearrange("p (b n) -> p b n", n=256)
            P.op(V, lambda e: e.tensor_copy(pbf_, pst), reads=ab(0, 4), writes=ab(4, 2))
            pT = av(6, 2).rearrange("p (k t) -> p k t", t=T)

            def ptr(e):
                for tb in range(TB):
                    for k in range(2):
                        ins = e.transpose(pbf(0)[:, (k * 4 + tb) * 128:(k * 4 + tb + 1) * 128],
                                          pbf_[:, tb, k * 128:(k + 1) * 128], ident[:])
                return ins
            P.op(PE, ptr, reads=ab(4, 2) + [B_const], writes=PB[0])
            P.op(V, lambda e: e.tensor_copy(av(6, 2), pbf(0)), reads=PB[0], writes=ab(6, 2))
            wp4 = load_wp(post_ple_w, av(24, 16, F32), ab(24, 16))
            wple_sb = av(8, 16).rearrange("p (k n) -> p k n", n=D)
            if "ple" not in wkeys:
                wkeys["ple"] = (wsc_alloc(8192), Buf("wk"))
                P.op(G, lambda e: e.dma_start(out=wple_sb, in_=wsrc_ple), writes=ab(8, 16), dma="dw")
                P.op(G, lambda e, off=wkeys["ple"][0]: e.dma_start(out=wsc_ap(off, 8192), in_=av(8, 16)),
                     reads=ab(8, 16), writes=[wkeys["ple"][1]], dma="dx")
            else:
                P.op(G, lambda e, off=wkeys["ple"][0]: e.dma_start(out=av(8, 16), in_=wsc_ap(off, 8192)),
                     reads=[wkeys["ple"][1]], writes=ab(8, 16), dma="dw")
            for cb in range(8):
                wt, wb_ = wload(("pg", cb), wsrc_pg[:, :, cb * 512:(cb + 1) * 512], KC, 512)
                for tb in range(TB):
                    gbk = (cb * TB + tb) % 4
                    pbk = 4 + (cb * TB + tb) % 4

                    def mm(e, wt=wt, tb=tb, gbk=gbk):
                        for kc in range(KC):
                            ins = e.matmul(psum[gbk][:], h3[:, kc, tb * 128:(tb + 1) * 128], wt[:, kc, :],
                                           start=(kc == 0), stop=(kc == KC - 1))
                        return ins
                    P.op(PE, mm, reads=wb_ + hT_tb(tb), writes=PB[gbk])

                    def mm2(e, tb=tb, pbk=pbk, cb=cb):
                        for k in range(2):
                            ins = e.matmul(psum[pbk][:], pT[:, k, tb * 128:(tb + 1) * 128],
                                           wple_sb[:, k, cb * 512:(cb + 1) * 512], start=(k == 0), stop=(k == 1))
                        return ins
                    P.op(PE, mm2, reads=ab(6, 18), writes=PB[pbk])
                    i = evnext()
                    P.op(A, lambda e, i=i, gbk=gbk: e.activation(ev[i][:], psum[gbk][:], AF.Sigmoid), reads=PB[gbk],
                         writes=[EV[i]])
                    P.op(V, lambda e, i=i, pbk=pbk: e.tensor_tensor(ev[i][:], ev[i][:], psum[pbk][:], ALU.mult),
                         reads=([EV[i]] + PB[pbk]), writes=[EV[i]])
                    i2 = evnext()
                    P.op(A, lambda e, i=i, i2=i2, tb=tb, cb=cb: e.activation(ev[i2][:], ev[i][:], AF.Square,
                                                                            accum_out=ssacc[:, tb, cb:cb + 1]),
                         reads=[EV[i]], writes=[EV[i2], B_ss])
                    P.op(V, lambda e, i=i, cb=cb, wp4=wp4: e.tensor_tensor(ev[i][:], ev[i][:],
                                                                           wp4[0][:, cb * 512:(cb + 1) * 512], ALU.mult),
                         reads=[EV[i]] + wp4[1], writes=[EV[i]])
                    P.op(SY, lambda e, i=i, tb=tb, cb=cb, accC=accC: e.dma_start(
                        out=accC(tb)[:, cb * 512:(cb + 1) * 512], in_=ev[i][:]), reads=[EV[i]],
                         writes=accC.bufs(tb), dma="ds")
            if dbg:
                P.op(SY, lambda e, ti=ti: e.dma_start(out=dbgp[ti], in_=pT), reads=ab(6, 2), writes=[Buf("x")], dma="ds")
                P.op(SY, lambda e, ti=ti: e.dma_start(out=dbgw[ti], in_=wple_sb), reads=ab(8, 16), writes=[Buf("x")], dma="ds")
                P.op(SY, lambda e, ti=ti: e.dma_start(out=dbgb[ti], in_=pbf_), reads=ab(4, 2), writes=[Buf("x")], dma="ds")
            st5 = {"a": [(av(40, 16, F32), ab(40, 16)), (av(56, 16, F32), ab(56, 16))],
                   "b": [(av(0, 16, F32), ab(0, 16)), (av(16, 16, F32), ab(16, 16))]}
            norm_pass(TB, x2rows, accC, None, post_ple_w, yrows, None, None, None, st5)
        if dbg:
            P.op(SY, lambda e: e.dma_start(out=sd, in_=S[:]), reads=B_S, writes=[Buf("x")], dma="ds")

        wflush()
        nc._wkeys = {k: v[0] for k, v in wkeys.items()}
        P.check()

        @block.tensor
        def _(e):
            P.replay("tensor", e, sems)

        @block.vector
        def _(e):
            P.replay("vector", e, sems)

        @block.scalar
        def _(e):
            P.replay("scalar", e, sems)

        @block.gpsimd
        def _(e):
            P.replay("gpsimd", e, sems)

        @block.sync
        def _(e):
            P.replay("sync", e, sems)
            for k in ("ds", "dx"):
                for i in range(DMA_ROT[k]):
                    if P.dcnt.get(f"{k}{i}", 0):
                        e.wait_ge(sems[f"{k}{i}"], P.dcnt[f"{k}{i}"])
    return nc


_NAMES = ["w_in", "w_out", "w_gate", "w_up", "w_down", "w_ple", "w_ple_gate"]


def make_in_maps(inputs, NT, NP, ncores, seq):
    g = lambda k: np.ascontiguousarray(np.asarray(inputs[k], dtype=np.float32))
    x = g("x")
    p = g("p")[0]
    shared = {
        "w_in": g("w_in")[0], "w_out": g("w_out")[0], "w_gate": g("w_gate")[0], "w_up": g("w_up")[0],
        "w_down": g("w_down")[0], "w_ple": g("w_ple")[0], "w_ple_gate": g("w_ple_gate")[0],
        "pre_mix_w": g("pre_mix_w")[0].reshape(32, 128), "pre_ffn_w": g("pre_ffn_w")[0].reshape(32, 128),
        "post_mix_w": g("post_mix_w")[0].reshape(1, D), "post_ffn_w": g("post_ffn_w")[0].reshape(1, D),
        "post_ple_w": g("post_ple_w")[0].reshape(1, D),
        "lb_param": g("lb_param").reshape(32, 128), "a_norm_w": g("a_norm_w")[0].reshape(16, 128),
        "gmlp_ln_w": g("gmlp_ln_w")[0].reshape(1, 2048), "gmlp_ln_b": g("gmlp_ln_b")[0].reshape(1, 2048),
        "w_spatial": g("w_spatial")[0], "b_spatial": g("b_spatial")[0].reshape(1, 2048),
    }
    maps = []
    nhalf = seq // NT
    for c in range(ncores):
        b, hf = divmod(c, nhalf)
        m = dict(shared)
        m["x"] = x[b, hf * NT:(hf + 1) * NT]
        m["p"] = p[b, hf * NT:(hf + 1) * NT]
        if hf == 0 or NP == 0:
            m["xp"] = np.zeros((max(NP, 128), D), np.float32)
        else:
            m["xp"] = x[b, hf * NT - NP:hf * NT]
        maps.append(m)
    return maps


def kernel(**inputs):
    NT, NP = 2048, 2048
    nc = build(NT, NP)
    maps = make_in_maps(inputs, NT, NP, 8, 4096)
    res = run_bass_kernel_spmd(nc, maps, core_ids=list(range(8)))
    out = np.empty((4, 4096, D), np.float32)
    for c in range(8):
        b, hf = divmod(c, 2)
        out[b, hf * NT:(hf + 1) * NT] = res.results[c]["y"]
    return out
```

```python
import numpy as np
import concourse.bass as bass
import concourse.mybir as mybir
from concourse.bass_utils import run_bass_kernel_spmd

F32, BF16 = mybir.dt.float32, mybir.dt.bfloat16
AF = mybir.ActivationFunctionType
ALU = mybir.AluOpType
AX = mybir.AxisListType

D = 4096
KC = 32
T = 512
TB = 4
DFF = 11008
NJ = 86
EPS = 1e-6
NSLAB = 86
ENGS = ("tensor", "vector", "scalar", "gpsimd", "sync")
OWN_DEPTH = 3
CACHE_W = True
WSTORE_Q = "gpsimd"
LATE_CACHE = ("g", "up", "pg")
DMA_ROT = {"dw": 8, "dl": 16, "ds": 16, "dx": 8}


class Tk:
    __slots__ = ("sem", "val")


class Buf:
    __slots__ = ("name", "w", "r")

    def __init__(self, name):
        self.name = name
        self.w = None
        self.r = {}


class Prog:
    def __init__(self):
        self.ops = {e: [] for e in ENGS}
        self.cnt = {e: 0 for e in ENGS}
        self.pend = {e: [] for e in ENGS}
        self.dcnt = {}
        self.dn = {}

    def op(self, eng, fn, reads=(), writes=(), signal=True, dma=None):
        deps = []
        for b in reads:
            if b.w is not None:
                deps.append(b.w)
        for b in writes:
            if b.w is not None:
                deps.append(b.w)
            deps.extend(b.r.values())
        t = Tk()
        if dma is not None:
            n = self.dn.get(dma, 0)
            self.dn[dma] = n + 1
            key = f"{dma}{n % DMA_ROT[dma]}"
            self.dcnt[key] = self.dcnt.get(key, 0) + 16
            t.sem = key
            t.val = self.dcnt[key]
            dma = key
        else:
            t.sem = eng
            if signal:
                self.cnt[eng] += 1
                t.val = self.cnt[eng]
                for p in self.pend[eng]:
                    p.val = t.val
                self.pend[eng] = []
            else:
                t.val = None
                self.pend[eng].append(t)
        for b in writes:
            b.w = t
            b.r = {}
        for b in reads:
            if b.w is not t:
                b.r[t.sem] = t
        self.ops[eng].append((fn, deps, t, signal, dma))
        return t

    def check(self):
        pos = {e: 0 for e in ENGS}
        val = {}
        progress = True
        while progress:
            progress = False
            for e in ENGS:
                while pos[e] < len(self.ops[e]):
                    fn, deps, t, signal, dma = self.ops[e][pos[e]]
                    ok = True
                    for d in deps:
                        assert d.val is not None, "unresolved ticket"
                        if d.sem == e and dma is None:
                            continue
                        if val.get(d.sem, 0) < d.val:
                            ok = False
                            break
                    if not ok:
                        break
                    if dma is not None:
                        val[dma] = val.get(dma, 0) + 16
                    elif signal:
                        val[e] = val.get(e, 0) + 1
                    pos[e] += 1
                    progress = True
        stuck = {e: (pos[e], len(self.ops[e])) for e in ENGS if pos[e] < len(self.ops[e])}
        assert not stuck, f"schedule deadlocks: {stuck}"

    def replay(self, eng, e, sems):
        seen = {}
        c = 0
        for fn, deps, t, signal, dma in self.ops[eng]:
            need = {}
            for d in deps:
                assert d.val is not None, "unresolved ticket"
                if d.sem == eng and dma is None:
                    if eng == "tensor":
                        continue
                    if d.val <= c - OWN_DEPTH:
                        continue
                if need.get(d.sem, 0) < d.val:
                    need[d.sem] = d.val
            for s, v in need.items():
                if seen.get(s, 0) < v:
                    e.wait_ge(sems[s], v)
                    seen[s] = v
            ins = fn(e)
            if dma is not None:
                ins.then_inc(sems[dma], 16)
            elif signal:
                ins.then_inc(sems[eng], 1)
                c += 1


def build(NT=2048, NP=2048, dbg=False):
    nc = bass.Bass("TRN2", target_bir_lowering=False)
    NTILES = NT // T
    PT = 1024 if NP >= 1024 else NP
    NPT = NP // PT if NP else 0

    def din(name, shape):
        return nc.dram_tensor(name, shape, F32, kind="ExternalInput").ap()

    x = din("x", [NT, D])
    xp = din("xp", [max(NP, 128), D])
    pin = din("p", [NT, 256])
    w_in = din("w_in", [D, 12288])
    w_out = din("w_out", [D, D])
    w_gate = din("w_gate", [D, DFF])
    w_up = din("w_up", [D, DFF])
    w_down = din("w_down", [DFF, D])
    w_ple = din("w_ple", [256, D])
    w_pg = din("w_ple_gate", [D, D])
    pre_mix_w = din("pre_mix_w", [32, 128])
    pre_ffn_w = din("pre_ffn_w", [32, 128])
    post_mix_w = din("post_mix_w", [1, D])
    post_ffn_w = din("post_ffn_w", [1, D])
    post_ple_w = din("post_ple_w", [1, D])
    lb_param = din("lb_param", [32, 128])
    a_norm_w = din("a_norm_w", [16, 128])
    ln_w = din("gmlp_ln_w", [1, 2048])
    ln_b = din("gmlp_ln_b", [1, 2048])
    w_sp = din("w_spatial", [16, 128, 128])
    b_sp = din("b_spatial", [1, 2048])
    y = nc.dram_tensor("y", [NT, D], F32, kind="ExternalOutput").ap()
    kind_dbg = dict(kind="ExternalOutput") if dbg else {}
    x1d = nc.dram_tensor("x1d", [NT, D], F32, **kind_dbg).ap()
    x2d = nc.dram_tensor("x2d", [NT, D], F32, **kind_dbg).ap()
    accd = nc.dram_tensor("accd", [2, T, D], F32, **kind_dbg).ap()
    rsd = nc.dram_tensor("rsd", [1, 2048], F32).ap()
    WSC_CH = 786432
    wscs = [nc.dram_tensor(f"wsc{i}", [128, WSC_CH], BF16, **kind_dbg).ap() for i in range(3)]
    if dbg:
        catd = nc.dram_tensor("catd", [NT // T, 128, 32, T], BF16, kind="ExternalOutput").ap()
        sd = nc.dram_tensor("sd", [128, 16, 128], F32, kind="ExternalOutput").ap()
        dbgp = nc.dram_tensor("dbgp", [NT // T, 128, 2, T], BF16, kind="ExternalOutput").ap()
        dbgw = nc.dram_tensor("dbgw", [NT // T, 128, 2, D], BF16, kind="ExternalOutput").ap()
        dbgb = nc.dram_tensor("dbgb", [NT // T, 128, 4, 256], BF16, kind="ExternalOutput").ap()

    P = Prog()
    from contextlib import ExitStack
    with ExitStack() as es:
        def sb(name, shape, dt):
            return es.enter_context(nc.sbuf_tensor(name, shape, dt))

        arena = sb("arena", [128, NSLAB * 512], BF16)
        hT = sb("hT", [128, KC, T], BF16)
        wsl = sb("wsl", [128, 8 * 4096], BF16)
        S = sb("S", [128, 16, 128], F32)
        WmT = sb("WmT", [128, 16, 128], BF16)
        cols = sb("cols", [128, 112], F32)
        lbc = sb("lbc", [128, 16], F32)
        oml = sb("oml", [128, 16], F32)
        ident = sb("ident", [128, 128], BF16)
        identf = sb("identf", [128, 128], F32)
        onesf = sb("onesf", [128, 128], F32)
        mask64 = sb("mask64", [64, 64], F32)
        ev = [sb(f"ev{i}", [128, 512], F32) for i in range(3)]
        ssacc = sb("ssacc", [128, TB, 8], F32)
        st = sb("st", [128, 64], F32)
        psum = [es.enter_context(nc.psum_tensor(f"ps{i}", [128, 512], F32)) for i in range(8)]
        sems = {e: es.enter_context(nc.semaphore("s_" + e)) for e in ENGS}
        for k, n in DMA_ROT.items():
            for i in range(n):
                sems[f"{k}{i}"] = es.enter_context(nc.semaphore(f"s_{k}{i}"))
        block = es.enter_context(nc.Block())

        AS = [Buf(f"a{j}") for j in range(NSLAB)]
        HBQ = [[Buf(f"h{j}_{tb}") for tb in range(TB)] for j in range(4)]
        HB = [b for q in HBQ for b in q]
        WB = [Buf(f"w{j}") for j in range(8)]
        PB = [[Buf(f"p{j}")] for j in range(8)]
        EV = [Buf(f"ev{j}") for j in range(3)]
        B_S = [Buf(f"S{h}") for h in range(16)]
        B_const = Buf("const")
        B_ss = Buf("ssacc")
        B_st = Buf("st")
        B_stn = [Buf("stn0"), Buf("stn1")]
        B_p5st = Buf("p5st")

        def av(s0, n, dt=BF16):
            ap = arena[:, s0 * 512:(s0 + n) * 512]
            if dt is F32:
                ap = ap.bitcast(F32)
            return ap

        def ab(s0, n):
            return AS[s0:s0 + n]

        def wv(s0, n, dt=BF16):
            ap = wsl[:, s0 * 4096:(s0 + n) * 4096]
            if dt is F32:
                ap = ap.bitcast(F32)
            return ap

        def pbf(i):
            return psum[i][:].bitcast(BF16)

        wpos = [0]
        wcap = [8]

        def walloc(n):
            a = 1 if n == 1 else (2 if n == 2 else 4)
            wpos[0] = (wpos[0] + a - 1) // a * a
            if wpos[0] + n > wcap[0]:
                wpos[0] = 0
            s0 = wpos[0]
            wpos[0] += n
            return s0

        wkeys = {}
        wuse = {}
        woff = [0]

        wpending = []

        def wflush():
            while wpending:
                flat, off, n, bufs, kb = wpending.pop(0)
                P.op(WSTORE_Q, lambda e, flat=flat, off=off, n=n: e.dma_start(out=wsc_ap(off, n), in_=flat),
                     reads=bufs, writes=[kb], dma="dx")

        def wsc_alloc(n):
            if woff[0] // WSC_CH != (woff[0] + n - 1) // WSC_CH:
                woff[0] = (woff[0] // WSC_CH + 1) * WSC_CH
            off = woff[0]
            woff[0] += n
            assert woff[0] <= 3 * WSC_CH
            return off

        def wsc_ap(off, n):
            return wscs[off // WSC_CH][:, off % WSC_CH:off % WSC_CH + n]

        def wload(key, src_ap, nk, ncols):
            n = nk * ncols
            nslots = (n + 4095) // 4096
            s0 = walloc(nslots)
            flat = wv(s0, nslots)[:, 0:n]
            dst = flat.rearrange("p (k n) -> p k n", n=ncols)
            bufs = WB[s0:s0 + nslots]
            if any(b in bufs for pend in wpending for b in pend[3]):
                wflush()
            if key not in wkeys:
                P.op("gpsimd", lambda e, dst=dst, src=src_ap: e.dma_start(out=dst, in_=src), writes=bufs, dma="dw")
                wflush()
                uses = wuse.get(key, 0)
                wuse[key] = uses + 1
                if key[0] not in LATE_CACHE or uses >= 1:
                    off = wsc_alloc(n)
                    kb = Buf("wk")
                    wkeys[key] = (off, kb)
                    wpending.append((flat, off, n, bufs, kb))
            else:
                off, kb = wkeys[key]
                P.op("gpsimd", lambda e, flat=flat, off=off, n=n: e.dma_start(out=flat, in_=wsc_ap(off, n)),
                     reads=[kb], writes=bufs, dma="dw")
                wflush()
            return dst, bufs

        evi = [0]

        def evnext():
            i = evi[0] % 3
            evi[0] += 1
            return i

        V = "vector"
        A = "scalar"
        G = "gpsimd"
        PE = "tensor"
        SY = "sync"

        P.op(G, lambda e: e.memset(identf[:], 1.0), writes=[B_const])
        P.op(G, lambda e: e.affine_select(identf[:], identf[:], [[-1, 128]], ALU.is_equal, 0.0, base=0,
                                          channel_multiplier=1), writes=[B_const])
        P.op(G, lambda e: e.tensor_copy(ident[:], identf[:]), writes=[B_const])
        P.op(G, lambda e: e.memset(onesf[:], 1.0), writes=[B_const])
        P.op(G, lambda e: e.memset(mask64[:], 1.0), writes=[B_const])
        P.op(G, lambda e: e.affine_select(mask64[:], mask64[:], [[1, 64]], ALU.is_ge, 0.0, base=0,
                                          channel_multiplier=-1), writes=[B_const])
        P.op(G, lambda e: e.memset(S[:], 0.0), writes=B_S)
        vrows = av(0, 1, F32)[0:112, 0:128]
        for r0, src, n in ((0, pre_mix_w, 32), (32, pre_ffn_w, 32), (64, lb_param, 32), (96, a_norm_w, 16)):
            P.op(SY, lambda e, r0=r0, src=src, n=n: e.dma_start(out=av(0, 1, F32)[r0:r0 + n, 0:128], in_=src),
                 writes=ab(0, 1), dma="dl")
        P.op(PE, lambda e: e.transpose(psum[0][:, 0:112], vrows, identf[0:112, 0:112]),
             reads=ab(0, 1) + [B_const], writes=PB[0])
        P.op(V, lambda e: e.tensor_copy(cols[:], psum[0][:, 0:112]), reads=PB[0], writes=[B_const])
        wcol_mix = cols[:, 0:32]
        wcol_ffn = cols[:, 32:64]
        anw = cols[:, 96:112]
        P.op(V, lambda e: e.tensor_tensor(st[:, 0:16], cols[:, 80:96], cols[:, 64:80], ALU.subtract),
             reads=[B_const], writes=[B_st])
        P.op(A, lambda e: e.activation(st[:, 16:32], st[:, 0:16], AF.Exp), reads=[B_st], writes=[B_st])
        P.op(V, lambda e: e.tensor_scalar_add(st[:, 32:48], st[:, 16:32], 1.0), reads=[B_st], writes=[B_st])
        P.op(V, lambda e: e.reciprocal(lbc[:], st[:, 32:48]), reads=[B_st], writes=[B_const])
        P.op(V, lambda e: e.tensor_tensor(oml[:], st[:, 16:32], lbc[:], ALU.mult), reads=[B_st, B_const],
             writes=[B_const])
        rscols = av(1, 1, F32)[:, 0:16]
        for g in range(16):
            wa = av(2 + (g % 2) * 2, 2, F32)[:, 0:128]
            wb_ = ab(2 + (g % 2) * 2, 2)
            P.op(SY, lambda e, wa=wa, g=g: e.dma_start(out=wa, in_=w_sp[g]), writes=wb_, dma="dl")
            P.op(G, lambda e, wa=wa: e.affine_select(wa, wa, [[-1, 128]], ALU.is_ge, 0.0, base=0,
                                                     channel_multiplier=1), reads=wb_, writes=wb_)
            P.op(V, lambda e, wa=wa, g=g: e.reduce_sum(rscols[:, g:g + 1], wa, AX.X), reads=wb_, writes=ab(1, 1))
            pb = 1 + g % 2
            P.op(PE, lambda e, wa=wa, pb=pb: e.transpose(psum[pb][:, 0:128], wa, identf[:]),
                 reads=wb_ + [B_const], writes=PB[pb])
            P.op(V, lambda e, g=g, pb=pb: e.tensor_copy(WmT[:, g, :], psum[pb][:, 0:128]), reads=PB[pb],
                 writes=[B_const])
        P.op(PE, lambda e: e.transpose(psum[3][0:16, 0:128], rscols, identf[:]), reads=ab(1, 1) + [B_const],
             writes=PB[3])
        rsrows = av(6, 1, F32)[0:16, 0:128]
        P.op(V, lambda e: e.tensor_copy(rsrows, psum[3][0:16, 0:128]), reads=PB[3], writes=ab(6, 1))
        B_rsd = Buf("rsd")
        P.op(SY, lambda e: e.dma_start(out=rsd.rearrange("o (g t) -> (o g) t", t=128), in_=rsrows), reads=ab(6, 1),
             writes=[B_rsd], dma="ds")

        def rstd_from_ss(ss_ap, n, out_ap, reads, extra_scale=1.0):
            P.op(A, lambda e: e.activation(out_ap, ss_ap, AF.Ln, scale=1.0 / n, bias=EPS), reads=[reads], writes=[reads])
            P.op(A, lambda e: e.activation(out_ap, out_ap, AF.Exp, scale=-0.5), reads=[reads], writes=[reads])

        def norm_pass(ntb, a_src, b_src, b_ss, wpost, dst, wcol, dstT, dstT_bufs, stage):
            for tb in range(ntb):
                a_ap, a_b = stage["a"][tb % 2]
                c0 = 52 + (tb % 2) * 4
                bst = B_stn[tb % 2]
                P.op(SY, lambda e, a_ap=a_ap, tb=tb: e.dma_start(out=a_ap, in_=a_src(tb)), reads=a_src.bufs(tb),
                     writes=a_b, dma="dl")
                if b_src is not None:
                    b_ap, b_b = stage["b"][tb % 2]
                    P.op(SY, lambda e, b_ap=b_ap, tb=tb: e.dma_start(out=b_ap, in_=b_src(tb)), reads=b_src.bufs(tb),
                         writes=b_b, dma="dl")
                    P.op(V, lambda e, tb=tb, c0=c0: e.reduce_sum(st[:, c0:c0 + 1], ssacc[:, tb, :], AX.X), reads=[B_ss],
                         writes=[bst])
                    rstd_from_ss(st[:, c0:c0 + 1], D, st[:, c0 + 1:c0 + 2], bst)
                    P.op(V, lambda e, a_ap=a_ap, b_ap=b_ap, c0=c0: e.scalar_tensor_tensor(
                        a_ap, b_ap, st[:, c0 + 1:c0 + 2], a_ap, ALU.mult, ALU.add),
                         reads=a_b + b_b + [bst], writes=a_b)
                if dst is not None:
                    P.op(SY, lambda e, a_ap=a_ap, tb=tb: e.dma_start(out=dst(tb), in_=a_ap), reads=a_b,
                         writes=dst.bufs(tb), dma="ds")
                if dstT is None:
                    continue
                xn_ap, xn_b = stage["xn"]
                if wcol is not None:
                    P.op(A, lambda e, a_ap=a_ap, c0=c0: e.activation(xn_ap, a_ap, AF.Square,
                                                                     accum_out=st[:, c0 + 2:c0 + 3]),
                         reads=a_b, writes=xn_b + [bst])
                    rstd_from_ss(st[:, c0 + 2:c0 + 3], D, st[:, c0 + 3:c0 + 4], bst)
                    P.op(A, lambda e, a_ap=a_ap, c0=c0: e.activation(xn_ap, a_ap, AF.Copy, scale=st[:, c0 + 3:c0 + 4]),
                         reads=a_b + [bst], writes=xn_b)
                else:
                    P.op(A, lambda e, a_ap=a_ap: e.activation(xn_ap, a_ap, AF.Copy), reads=a_b, writes=xn_b)
                for q in range(4):
                    pb = (tb % 2) * 4 + q
                    def tr(e, q=q, pb=pb):
                        for k in range(8):
                            kc = q * 8 + k
                            ins = e.transpose(pbf(pb)[:, k * 128:(k + 1) * 128], xn_ap[:, kc * 128:(kc + 1) * 128],
                                              ident[:])
                        return ins
                    P.op(PE, tr, reads=xn_b + [B_const], writes=PB[pb])
                    src3 = pbf(pb).rearrange("p (k t) -> p k t", t=128)
                    dst3 = dstT(q, tb)
                    if wcol is not None:
                        wc = wcol[:, q * 8:(q + 1) * 8].unsqueeze(2).to_broadcast([128, 8, 128])
                        P.op(V, lambda e, src3=src3, dst3=dst3, wc=wc: e.tensor_tensor(dst3, src3, wc, ALU.mult),
                             reads=(PB[pb] + [B_const]), writes=dstT_bufs(q, tb))
                    else:
                        P.op(V, lambda e, src3=src3, dst3=dst3: e.tensor_copy(dst3, src3), reads=PB[pb],
                             writes=dstT_bufs(q, tb))

        class Rows:
            def __init__(self, ap, row0, name):
                self.ap = ap
                self.row0 = row0
                self.b = {}
                self.name = name

            def __call__(self, tb):
                r = self.row0 + tb * 128
                return self.ap[r:r + 128, :]

            def bufs(self, tb):
                return [self.b.setdefault(tb, Buf(f"{self.name}{tb}"))]

        class Head:
            def __init__(self, h, proj, ntok, ch, full, fe0, cl0, cat_slab):
                self.h, self.proj, self.ntok, self.ch, self.full = h, proj, ntok, ch, full
                self.nch = ntok // ch
                self.fe_cur = [fe0]
                self.cl_cur = [cl0]
                self.cat_slab = cat_slab

            def _t(self, cur, n, dt):
                a0 = cur[0]
                cur[0] += n
                assert cur[0] <= NSLAB
                return av(a0, n, dt), ab(a0, n)

            def uloc(self, c):
                if self.nch == 8:
                    return 6 + c // 4, (c % 4) * 128, PB[6 + c // 4]
                return 7, c * 128, PB[7]

            def fe_elem(self):
                h, proj, ntok, ch, nch, full = self.h, self.proj, self.ntok, self.ch, self.nch, self.full
                fe = lambda n=2, dt=F32: self._t(self.fe_cur, n, dt)
                cl = lambda n=2, dt=F32: self._t(self.cl_cur, n, dt)
                fb, ib = proj["f"], proj["i"]
                ef, ef_b = fe()
                ef = ef[:, 0:ntok]
                P.op(A, lambda e: e.activation(ef, psum[fb][:, 0:ntok], AF.Sigmoid, scale=-1.0), reads=PB[fb],
                     writes=ef_b)
                iTb, iTb_b = fe(1, BF16)
                iTb = iTb[:, 0:ntok]
                P.op(A, lambda e: e.activation(iTb, psum[ib][:, 0:ntok], AF.Copy), reads=PB[ib], writes=iTb_b)
                if full:
                    qb, gb = proj["q"], proj["g"]
                    eq_, eq_b = fe()
                    eq_ = eq_[:, 0:ntok]
                    P.op(A, lambda e: e.activation(eq_, psum[qb][:, 0:ntok], AF.Silu), reads=PB[qb], writes=eq_b)
                    eg, eg_b = cl()
                    eg = eg[:, 0:ntok]
                    qs, qs_b, sg, sg_b = eq_, eq_b, eg, eg_b
                    self.sg, self.sg_b = sg, sg_b
                self._loc = dict(locals())

            def fe_b(self):
                L = self._loc
                h, proj, ntok, ch, nch, full = self.h, self.proj, self.ntok, self.ch, self.nch, self.full
                fe, cl, ef, ef_b, iTb, iTb_b = L["fe"], L["cl"], L["ef"], L["ef_b"], L["iTb"], L["iTb_b"]
                if full:
                    gb, eg, eg_b, qs, qs_b = L["gb"], L["eg"], L["eg_b"], L["qs"], L["qs_b"]
                    P.op(A, lambda e: e.activation(eg, psum[gb][:, 0:ntok], AF.Silu), reads=PB[gb], writes=eg_b)
                rf, rf_b = fe()
                rf = rf[:, 0:ntok]
                kT, kT_b = fe()
                kT = kT[:, 0:ntok]
                P.op(V, lambda e: e.tensor_scalar(kT, ef, oml[:, h:h + 1], None, ALU.mult), reads=ef_b + [B_const],
                     writes=kT_b)
                lf = ef
                P.op(A, lambda e: e.activation(lf, kT, AF.Ln, scale=-1.0, bias=1.0), reads=kT_b, writes=ef_b)
                Bp, Bp_b = fe(3)
                Bp = Bp[:, 0:ntok + 1]
                P.op(V, lambda e: e.memset(Bp[:, 0:1], 0.0), writes=Bp_b)
                P.op(V, lambda e: e.tensor_tensor_scan(Bp[:, 1:ntok + 1], onesf[:, 0:1].to_broadcast([128, ntok]), lf,
                                                       0.0, ALU.mult, ALU.add), reads=ef_b + [B_const], writes=Bp_b)
                brel, brel_b = rf, rf_b
                b3 = brel.rearrange("p (c t) -> p c t", t=ch)
                P.op(V, lambda e: e.tensor_tensor(b3, Bp[:, 1:ntok + 1].rearrange("p (c t) -> p c t", t=ch),
                                                  Bp[:, 0:ntok].rearrange("p (c t) -> p c t", t=ch)[:, :, 0:1]
                                                  .to_broadcast([128, nch, ch]), ALU.subtract),
                     reads=Bp_b, writes=brel_b)
                d2, d2_b = fe()
                d2 = d2[:, 0:ntok]
                d23 = d2.rearrange("p (c t) -> p c t", t=ch)
                P.op(V, lambda e: e.tensor_tensor(d23, b3[:, :, ch - 1:ch].to_broadcast([128, nch, ch]), b3,
                                                  ALU.subtract), reads=brel_b, writes=d2_b)
                P.op(A, lambda e: e.activation(d2, d2, AF.Exp), reads=d2_b, writes=d2_b)
                edec, edec_b = cl(1)
                edec = edec[:, 0:nch]
                P.op(A, lambda e: e.activation(edec.unsqueeze(2), b3[:, :, ch - 1:ch], AF.Exp), reads=brel_b,
                     writes=edec_b)
                self.edec, self.edec_b = edec, edec_b
                khT, khT_b = fe(1, BF16)
                khT = khT[:, 0:ntok]
                P.op(V, lambda e: e.tensor_tensor(khT, kT, d2, ALU.mult), reads=kT_b + d2_b, writes=khT_b)
                if full:
                    ex, ex_b = fe()
                    ex = ex[:, 0:ntok]
                    P.op(A, lambda e: e.activation(ex, brel, AF.Exp), reads=brel_b, writes=ex_b)
                    qt, qt_b = cl(1, BF16)
                    qt = qt[:, 0:ntok]
                    P.op(V, lambda e: e.tensor_tensor(qt, qs, ex, ALU.mult), reads=qs_b + ex_b, writes=qt_b)
                    ex2, ex2_b = fe()
                    ex2 = ex2[:, 0:ntok]
                    P.op(A, lambda e: e.activation(ex2, brel, AF.Exp, scale=-1.0), reads=brel_b, writes=ex2_b)
                    kt, kt_b = fe(1, BF16)
                    kt = kt[:, 0:ntok]
                    P.op(V, lambda e: e.tensor_tensor(kt, kT, ex2, ALU.mult), reads=kT_b + ex2_b, writes=kt_b)
                    self.qt, self.qt_b, self.kt, self.kt_b = qt, qt_b, kt, kt_b
                    sT, sT_b = cl(1, BF16)
                    self.sTm = [sT[0:ch, i * ch:(i + 1) * ch] for i in range(nch)]
                    self.sTm_b = sT_b
                    s0_, s0_b = cl(1, BF16)
                    s1_, s1_b = cl(1, BF16)
                    self.Sbf = [(s0_[:, 0:128], s0_b), (s1_[:, 0:128], s1_b)]
                    self.osq, self.osq_b = cl()
                    self.rs, self.rs_b = cl()
                nsl = (nch * 128 * 2 + 1023) // 1024
                vt, self.vtok_b = cl(nsl, BF16)
                self.vtok = vt[0:ch, 0:nch * 128].rearrange("p (c v) -> p c v", v=128)
                ktk, self.ktok_b = fe(nsl, BF16)
                self.ktok = ktk[0:ch, 0:nch * 128].rearrange("p (c v) -> p c v", v=128)
                self.iTb, self.iTb_b, self.khT, self.khT_b = iTb, iTb_b, khT, khT_b

            def fe_pe(self):
                h, ntok, ch, nch, full = self.h, self.ntok, self.ch, self.nch, self.full
                pbv, pbk = (6, 7) if full else (5, 6)
                for (srcT, srcT_b, dtok, dtok_b, pb) in ((self.iTb, self.iTb_b, self.vtok, self.vtok_b, pbv),
                                                        (self.khT, self.khT_b, self.ktok, self.ktok_b, pbk)):
                    def tr(e, srcT=srcT, pb=pb):
                        for c in range(nch):
                            ins = e.transpose(pbf(pb)[0:ch, c * 128:(c + 1) * 128], srcT[:, c * ch:(c + 1) * ch],
                                              ident[:])
                        return ins
                    P.op(PE, tr, reads=srcT_b + [B_const], writes=PB[pb])
                    P.op(V, lambda e, dtok=dtok, pb=pb: e.tensor_copy(
                        dtok, pbf(pb)[0:ch, 0:nch * 128].rearrange("p (c v) -> p c v", v=128)), reads=PB[pb],
                         writes=dtok_b)
                if full:
                    P.op(A, lambda e: e.activation(self.Sbf[0][0], S[:, h, :], AF.Copy), reads=[B_S[h]],
                         writes=self.Sbf[0][1])
                if full:
                    for c in range(nch):
                        cs = slice(c * ch, (c + 1) * ch)
                        sps = psum[5][0:ch, c * ch:(c + 1) * ch]
                        P.op(PE, lambda e, cs=cs, sps=sps: e.matmul(sps, self.kt[:, cs], self.qt[:, cs], start=True,
                                                                    stop=True),
                             reads=self.kt_b + self.qt_b, writes=PB[5])
                order = list(range(nch))
                if nch == 8:
                    order = [0, 1, 2, 3, 4, 5, 6, 7]
                for c in order:
                    ubk, ucol, ub_ = self.uloc(c)
                    P.op(PE, lambda e, c=c, ubk=ubk, ucol=ucol: e.matmul(psum[ubk][:, ucol:ucol + 128],
                                                                         self.ktok[:, c, :], self.vtok[:, c, :],
                                                                         start=True, stop=True),
                         reads=self.ktok_b + self.vtok_b, writes=ub_)
                if full:
                    for c in range(nch):
                        sps = psum[5][0:ch, c * ch:(c + 1) * ch]
                        P.op(V, lambda e, sps=sps, c=c: e.tensor_tensor(self.sTm[c], sps, mask64[0:ch, 0:ch],
                                                                        ALU.mult),
                             reads=PB[5] + [B_const], writes=self.sTm_b)

            def chain_step(self, c):
                h, ch, nch, full = self.h, self.ch, self.nch, self.full
                cs = slice(c * ch, (c + 1) * ch)
                if full:
                    sb_ap, sb_b = self.Sbf[c % 2]

                    def om(e, c=c, cs=cs, sb_ap=sb_ap):
                        e.matmul(psum[4][:, cs], self.vtok[:, c, :], self.sTm[c], start=True, stop=False)
                        return e.matmul(psum[4][:, cs], sb_ap, self.qt[:, cs], start=False, stop=True)
                    P.op(PE, om, reads=self.vtok_b + self.sTm_b + sb_b + self.qt_b, writes=PB[4])
                ubk, ucol, ub_ = self.uloc(c)
                P.op(V, lambda e, c=c, ubk=ubk, ucol=ucol: e.scalar_tensor_tensor(
                    S[:, h, :], S[:, h, :], self.edec[:, c:c + 1], psum[ubk][:, ucol:ucol + 128], ALU.mult, ALU.add),
                     reads=ub_ + [B_S[h]] + self.edec_b, writes=[B_S[h]])
                if full and c < nch - 1:
                    nb_ap, nb_b = self.Sbf[(c + 1) % 2]
                    P.op(A, lambda e, nb_ap=nb_ap: e.activation(nb_ap, S[:, h, :], AF.Copy), reads=[B_S[h]],
                         writes=nb_b)

            def tail_a(self):
                if not self.full:
                    return
                ntok = self.ntok
                osq, osq_b = self.osq[:, 0:ntok], self.osq_b
                P.op(A, lambda e: e.activation(osq, psum[4][:, 0:ntok], AF.Square), reads=PB[4], writes=osq_b)

            def tail_b(self):
                if not self.full:
                    return
                h, ntok = self.h, self.ntok
                osq, osq_b, rs_, rs_b = self.osq[:, 0:ntok], self.osq_b, self.rs[:, 0:ntok], self.rs_b
                P.op(PE, lambda e: e.matmul(psum[5][:, 0:ntok], onesf[:], osq, start=True, stop=True),
                     reads=osq_b + [B_const], writes=PB[5])
                P.op(A, lambda e: e.activation(rs_, psum[5][:, 0:ntok], AF.Ln, scale=1.0 / 128, bias=EPS),
                     reads=PB[5], writes=rs_b)
                P.op(A, lambda e: e.activation(rs_, rs_, AF.Exp, scale=-0.5), reads=rs_b, writes=rs_b)
                P.op(V, lambda e: e.tensor_tensor(rs_, psum[4][:, 0:ntok], rs_, ALU.mult), reads=rs_b + PB[4],
                     writes=rs_b)
                cat = av(self.cat_slab, 1)[:, 0:ntok]
                P.op(V, lambda e: e.scalar_tensor_tensor(cat, rs_, anw[:, h:h + 1], self.sg, ALU.mult, ALU.mult),
                     reads=rs_b + self.sg_b + [B_const], writes=ab(self.cat_slab, 1))

        wsrc_in = w_in.rearrange("(k p) n -> p k n", p=128)
        for ph in range(NPT):
            nslab_h = KC * PT // 512
            hp3 = av(0, nslab_h).rearrange("p (k t) -> p k t", t=PT)
            ntb = PT // 128
            stage = {"a": [(wv(0, 2, F32), WB[0:2]), (wv(2, 2, F32), WB[2:4])], "xn": (wv(4, 1), WB[4:5])}
            src = Rows(xp, ph * PT, "xp")
            wflush()
            gsz = 8 * PT // 512

            def dstT(q, tb, hp3=hp3):
                return hp3[:, q * 8:(q + 1) * 8, tb * 128:(tb + 1) * 128]

            def dstT_bufs(q, tb, gsz=gsz):
                return ab(q * gsz, gsz)
            norm_pass(ntb, src, None, None, None, None, wcol_mix, dstT, dstT_bufs, stage)
            ntg = PT // 512
            t0 = nslab_h
            units = [(h, tg) for h in range(16) for tg in range(ntg)]
            pw = {}

            def emit_proj(u, hp3=hp3):
                h, tg = units[u]
                if h % 2 == 0 and tg == 0:
                    hp_ = h // 2
                    pw["f"] = wload(("pf", hp_), wsrc_in[:, :, 2048 + hp_ * 256:2048 + (hp_ + 1) * 256], KC, 256)
                    pw["i"] = wload(("pi", hp_), wsrc_in[:, :, 4096 + hp_ * 256:4096 + (hp_ + 1) * 256], KC, 256)
                hh = h % 2
                for nm, bank in (("f", (u % 2) * 2), ("i", (u % 2) * 2 + 1)):
                    wt, wb_ = pw[nm]

                    def mm(e, wt=wt, hh=hh, tg=tg, bank=bank):
                        for kc in range(KC):
                            ins = e.matmul(psum[bank][:, :], wt[:, kc, hh * 128:(hh + 1) * 128],
                                           hp3[:, kc, tg * 512:(tg + 1) * 512], start=(kc == 0), stop=(kc == KC - 1))
                        return ins
                    P.op(PE, mm, reads=wb_ + ab(0, nslab_h), writes=PB[bank])

            def mk_head(u):
                h, tg = units[u]
                return Head(h, {"f": (u % 2) * 2, "i": (u % 2) * 2 + 1}, 512, 128, False, t0, t0 + 14 + (u % 2) * 2, None)
            emit_proj(0)
            H = mk_head(0)
            H.fe_elem()
            H.fe_b()
            for u in range(len(units)):
                if u + 1 < len(units):
                    emit_proj(u + 1)
                H.fe_pe()
                for c in range(4):
                    H.chain_step(c)
                if u + 1 < len(units):
                    H = mk_head(u + 1)
                    H.fe_elem()
                    H.fe_b()

        wsrc_out = w_out.rearrange("(k p) n -> p k n", p=128)
        wsrc_g = w_gate.rearrange("(k p) n -> p k n", p=128)
        wsrc_u = w_up.rearrange("(k p) n -> p k n", p=128)
        wsrc_d = w_down.rearrange("(j p) n -> p j n", p=128)
        wsrc_pg = w_pg.rearrange("(k p) n -> p k n", p=128)
        wsrc_ple = w_ple.rearrange("(k p) n -> p k n", p=128)
        h3 = hT[:]

        def hT_dst(q, tb):
            return h3[:, q * 8:(q + 1) * 8, tb * 128:(tb + 1) * 128]

        def hT_bufs(q, tb):
            return [HBQ[q][tb]]

        def hT_tb(tb):
            return [HBQ[q][tb] for q in range(4)]

        def acc_evac(bank, tb, cb, accrows, wp):
            wp_ap, wp_b = wp
            i = evnext()
            P.op(A, lambda e: e.activation(ev[i][:], psum[bank][:], AF.Square, accum_out=ssacc[:, tb, cb:cb + 1]),
                 reads=PB[bank], writes=[EV[i], B_ss])
            P.op(V, lambda e: e.tensor_tensor(ev[i][:], psum[bank][:], wp_ap[:, cb * 512:(cb + 1) * 512], ALU.mult),
                 reads=PB[bank] + wp_b, writes=[EV[i]])
            P.op(SY, lambda e: e.dma_start(out=accrows(tb)[:, cb * 512:(cb + 1) * 512], in_=ev[i][:]), reads=[EV[i]],
                 writes=accrows.bufs(tb), dma="ds")

        def load_wp(vec, ap, bufs):
            wflush()
            P.op(SY, lambda e: e.dma_start(out=ap, in_=vec.partition_broadcast(128)), writes=bufs, dma="dl")
            return ap, bufs

        def std_stage():
            return {"a": [(av(0, 16, F32), ab(0, 16)), (av(16, 16, F32), ab(16, 16))],
                    "b": [(av(32, 16, F32), ab(32, 16)), (av(48, 16, F32), ab(48, 16))],
                    "xn": (av(64, 8), ab(64, 8))}

        pending_p5 = []
        for ti in range(NTILES):
            r0 = ti * T
            xrows = Rows(x, r0, f"x{ti}_")
            x1rows = Rows(x1d, r0, f"x1_{ti}_")
            x2rows = Rows(x2d, r0, f"x2_{ti}_")
            yrows = Rows(y, r0, f"y{ti}_")
            accA = Rows(accd[ti % 2], 0, f"acc{ti}_")
            norm_pass(TB, xrows, None, None, None, None, wcol_mix, hT_dst, hT_bufs, std_stage())
            while pending_p5:
                pending_p5.pop(0)()
            GV0 = 32
            gv = av(GV0, 16).rearrange("p (b n) -> p b n", n=2048)
            lnw_bc = av(48, 8, F32)
            P.op(SY, lambda e: e.dma_start(out=lnw_bc, in_=ln_w.partition_broadcast(128)), writes=ab(48, 8), dma="dl")
            l2 = av(56, 8, F32)[0:2, :]
            r2 = av(64, 8, F32)[0:2, :]
            P.op(V, lambda e: e.memset(l2, 1.0), writes=ab(56, 8))
            P.op(SY, lambda e: e.dma_start(out=av(56, 8, F32)[0:1, :], in_=ln_b), writes=ab(56, 8), dma="dl")
            P.op(SY, lambda e: e.dma_start(out=av(64, 8, F32)[0:1, :], in_=rsd), reads=[B_rsd], writes=ab(64, 8),
                 dma="dl")
            P.op(SY, lambda e: e.dma_start(out=av(64, 8, F32)[1:2, :], in_=b_sp), writes=ab(64, 8), dma="dl")
            for cg in range(4):
                wt, wb_ = wload(("v", cg), wsrc_in[:, :, 10240 + cg * 512:10240 + (cg + 1) * 512], KC, 512)
                for tb in range(TB):
                    bank = (cg * TB + tb) % 4

                    def mm(e, wt=wt, tb=tb, bank=bank):
                        for kc in range(KC):
                            ins = e.matmul(psum[bank][:], h3[:, kc, tb * 128:(tb + 1) * 128], wt[:, kc, :],
                                           start=(kc == 0), stop=(kc == KC - 1))
                        return ins
                    P.op(PE, mm, reads=wb_ + hT_tb(tb), writes=PB[bank])
                    gslab = ab(GV0 + tb * 4 + cg, 1)
                    P.op(A, lambda e, tb=tb, cg=cg, bank=bank: e.activation(
                        gv[:, tb, cg * 512:(cg + 1) * 512], psum[bank][:], AF.Gelu,
                        accum_out=st[:, 8 + tb * 4 + cg:9 + tb * 4 + cg]), reads=PB[bank], writes=gslab + [B_st])
                    i = evnext()
                    P.op(A, lambda e, tb=tb, cg=cg, i=i: e.activation(
                        ev[i][:], gv[:, tb, cg * 512:(cg + 1) * 512], AF.Square,
                        accum_out=st[:, 24 + tb * 4 + cg:25 + tb * 4 + cg]), reads=gslab, writes=[EV[i], B_st])
            s1 = st[:, 8:24].rearrange("p (b c) -> p b c", c=4)
            s2 = st[:, 24:40].rearrange("p (b c) -> p b c", c=4)
            P.op(V, lambda e: e.reduce_sum(st[:, 40:44], s1, AX.X), reads=[B_st], writes=[B_st])
            P.op(V, lambda e: e.reduce_sum(st[:, 44:48], s2, AX.X), reads=[B_st], writes=[B_st])
            P.op(V, lambda e: e.tensor_scalar_mul(st[:, 40:44], st[:, 40:44], 1.0 / 2048), reads=[B_st], writes=[B_st])
            P.op(V, lambda e: e.tensor_tensor(st[:, 48:52], st[:, 40:44], st[:, 40:44], ALU.mult), reads=[B_st],
                 writes=[B_st])
            P.op(V, lambda e: e.scalar_tensor_tensor(st[:, 44:48], st[:, 44:48], 1.0 / 2048, st[:, 48:52], ALU.mult,
                                                     ALU.subtract), reads=[B_st], writes=[B_st])
            P.op(A, lambda e: e.activation(st[:, 44:48], st[:, 44:48], AF.Ln, bias=EPS), reads=[B_st], writes=[B_st])
            P.op(A, lambda e: e.activation(st[:, 44:48], st[:, 44:48], AF.Exp, scale=-0.5), reads=[B_st],
                 writes=[B_st])
            P.op(V, lambda e: e.scalar_tensor_tensor(st[:, 48:52], st[:, 40:44], -1.0, st[:, 44:48], ALU.mult,
                                                     ALU.mult), reads=[B_st], writes=[B_st])
            for tb in range(TB):
                gs = ab(GV0 + tb * 4, 4)
                P.op(V, lambda e, tb=tb: e.tensor_scalar(gv[:, tb, :], gv[:, tb, :], st[:, 44 + tb:45 + tb],
                                                         st[:, 48 + tb:49 + tb], ALU.mult, ALU.add),
                     reads=gs + [B_st], writes=gs)
                P.op(V, lambda e, tb=tb: e.tensor_tensor(gv[:, tb, :], gv[:, tb, :], lnw_bc, ALU.mult),
                     reads=gs + ab(48, 8), writes=gs)
            for uq in range(4):
                wt, wb_ = wload(("u", uq), wsrc_in[:, :, 8192 + uq * 512:8192 + (uq + 1) * 512], KC, 512)
                for gg in range(4):
                    g = uq * 4 + gg
                    ub = 4 + g % 2
                    zb = 6 + g % 2

                    def mm(e, wt=wt, gg=gg, ub=ub):
                        for kc in range(KC):
                            ins = e.matmul(psum[ub][:], wt[:, kc, gg * 128:(gg + 1) * 128], h3[:, kc, :],
                                           start=(kc == 0), stop=(kc == KC - 1))
                        return ins
                    P.op(PE, mm, reads=wb_ + HB, writes=PB[ub])
                    gu = av(72 + g % 2, 1)
                    gu_b = ab(72 + g % 2, 1)
                    P.op(A, lambda e, gu=gu, ub=ub: e.activation(gu, psum[ub][:], AF.Gelu), reads=PB[ub],
                         writes=gu_b)

                    def zm(e, g=g, zb=zb):
                        for tb in range(TB):
                            e.matmul(psum[zb][:, tb * 128:(tb + 1) * 128], gv[:, tb, g * 128:(g + 1) * 128],
                                     WmT[:, g, :], start=True, stop=False)
                            ins = e.matmul(psum[zb][:, tb * 128:(tb + 1) * 128], l2[:, g * 128:(g + 1) * 128],
                                           r2[:, g * 128:(g + 1) * 128], start=False, stop=True)
                        return ins
                    P.op(PE, zm, reads=ab(GV0, 16) + ab(56, 16) + [B_const], writes=PB[zb])
                    P.op(V, lambda e, gu=gu, zb=zb, g=g: e.tensor_tensor(av(16 + g, 1), psum[zb][:], gu, ALU.mult),
                         reads=PB[zb] + gu_b, writes=ab(16 + g, 1))
            banks = {"q": 0, "f": 1, "i": 2, "g": 3}

            def load_head_w(h):
                return {nm: wload(("hd", h, nm), wsrc_in[:, :, c0 + h * 128:c0 + (h + 1) * 128], KC, 128)
                        for nm, c0 in (("f", 2048), ("i", 4096), ("q", 0), ("g", 6144))}

            def proj_piece(wt, wb_, bank, k0, k1):
                def mm(e):
                    for kc in range(k0, k1):
                        ins = e.matmul(psum[bank][:], wt[:, kc, :], h3[:, kc, :], start=(kc == 0), stop=(kc == KC - 1))
                    return ins
                P.op(PE, mm, reads=wb_ + HB, writes=PB[bank], signal=(k1 == KC))
            wts = load_head_w(0)
            for nm in ("f", "i", "q", "g"):
                proj_piece(*wts[nm], banks[nm], 0, KC)
            H = Head(0, banks, T, 64, True, 32, 54, 0)
            H.fe_elem()
            H.fe_b()
            Hprev = None
            for h in range(16):
                nxt = h + 1 < 16
                if nxt:
                    wts = load_head_w(h + 1)
                    proj_piece(*wts["f"], banks["f"], 0, KC)
                    proj_piece(*wts["i"], banks["i"], 0, KC)
                if Hprev is not None:
                    Hprev.tail_b()
                if nxt:
                    proj_piece(*wts["q"], banks["q"], 0, KC)
                H.fe_pe()
                if nxt:
                    Hn = Head(h + 1, banks, T, 64, True, 32, 54 + ((h + 1) % 2) * 16, h + 1)
                    Hn.fe_elem()
                for c in range(8):
                    if nxt:
                        proj_piece(*wts["g"], banks["g"], c * 4, c * 4 + 4)
                    H.chain_step(c)
                H.tail_a()
                if nxt:
                    Hn.fe_b()
                Hprev = H
                if nxt:
                    H = Hn
            Hprev.tail_b()
            if dbg:
                P.op(SY, lambda e, ti=ti: e.dma_start(out=catd[ti], in_=av(0, 32).rearrange("p (k t) -> p k t", t=T)),
                     reads=ab(0, 32), writes=[Buf("x")], dma="ds")
            cat3 = av(0, 32).rearrange("p (k t) -> p k t", t=T)

            def mm_A(src3, src_bufs, nk, wsrc, accrows, wp, wname, post=None):
                for cb in range(8):
                    wt, wb_ = wload((wname, cb), wsrc[:, :, cb * 512:(cb + 1) * 512], nk, 512)
                    for tb in range(TB):
                        bank = (cb % 2) * 4 + tb

                        def mm(e, wt=wt, tb=tb, bank=bank):
                            for kc in range(nk):
                                ins = e.matmul(psum[bank][:], src3[:, kc, tb * 128:(tb + 1) * 128], wt[:, kc, :],
                                               start=(kc == 0), stop=(kc == nk - 1))
                            return ins
                        P.op(PE, mm, reads=wb_ + src_bufs, writes=PB[bank])
                        if post is None:
                            acc_evac(bank, tb, cb, accrows, wp)
                        else:
                            post(bank, tb, cb)
            wp2 = load_wp(post_mix_w, av(32, 16, F32), ab(32, 16))
            mm_A(cat3, ab(0, 32), KC, wsrc_out, accA, wp2, "out")
            norm_pass(TB, xrows, accA, None, post_mix_w, x1rows, wcol_ffn, hT_dst, hT_bufs, std_stage())
            for jq in range(43):
                wtg, wg_b = wload(("g", jq), wsrc_g[:, :, jq * 256:(jq + 1) * 256], KC, 256)
                wtu, wu_b = wload(("up", jq), wsrc_u[:, :, jq * 256:(jq + 1) * 256], KC, 256)
                for jj in range(2):
                    j = jq * 2 + jj
                    gbk = (j % 2) * 2
                    ubk = gbk + 1
                    for wt, wb_, bank in ((wtg, wg_b, gbk), (wtu, wu_b, ubk)):
                        def mm(e, wt=wt, jj=jj, bank=bank):
                            for kc in range(KC):
                                ins = e.matmul(psum[bank][:], wt[:, kc, jj * 128:(jj + 1) * 128], h3[:, kc, :],
                                               start=(kc == 0), stop=(kc == KC - 1))
                            return ins
                        P.op(PE, mm, reads=wb_ + HB, writes=PB[bank])
                    i = evnext()
                    P.op(A, lambda e, i=i, gbk=gbk: e.activation(ev[i][:], psum[gbk][:], AF.Silu), reads=PB[gbk],
                         writes=[EV[i]])
                    P.op(V, lambda e, i=i, ubk=ubk, j=j: e.tensor_tensor(av(j, 1), ev[i][:], psum[ubk][:], ALU.mult),
                         reads=([EV[i]] + PB[ubk]), writes=ab(j, 1))
            accB = Rows(accd[ti % 2], 0, f"accb{ti}_")
            accB.b = accA.b
            act3 = av(0, NJ).rearrange("p (j t) -> p j t", t=T)
            wcap[0] = 6
            if wpos[0] > 6:
                wpos[0] = 0
            wp3 = load_wp(post_ffn_w, wv(6, 2, F32), WB[6:8])
            for cb in range(8):
                for jg in range(6):
                    nj = 16 if jg < 5 else 6
                    wt, wb_ = wload(("d", cb, jg), wsrc_d[:, jg * 16:jg * 16 + nj, cb * 512:(cb + 1) * 512], nj, 512)
                    for tb in range(TB):
                        bank = (cb % 2) * 4 + tb

                        def mm(e, wt=wt, tb=tb, bank=bank, jg=jg, nj=nj):
                            for jj in range(nj):
                                j = jg * 16 + jj
                                ins = e.matmul(psum[bank][:], act3[:, j, tb * 128:(tb + 1) * 128], wt[:, jj, :],
                                               start=(j == 0), stop=(j == NJ - 1))
                            return ins
                        P.op(PE, mm, reads=wb_ + ab(jg * 16, nj), writes=PB[bank], signal=True)
                        if jg == 5:
                            acc_evac(bank, tb, cb, accB, wp3)
            wcap[0] = 8
            norm_pass(TB, x1rows, accB, None, post_ffn_w, x2rows, None, hT_dst, hT_bufs, std_stage())
            accC = Rows(accd[ti % 2], 0, f"accc{ti}_")
            accC.b = accA.b
            pst = av(0, 4, F32).rearrange("p (b n) -> p b n", n=256)
            P.op(SY, lambda e, r0=r0: e.dma_start(out=pst, in_=pin[r0:r0 + T, :].rearrange("(b p) n -> p b n", p=128)),
                 writes=ab(0, 4), dma="dl")
            pbf_ = av(4, 2).rearrange("p (b n) -> p b n", n=256)
            P.op(V, lambda e: e.tensor_copy(pbf_, pst), reads=ab(0, 4), writes=ab(4, 2))
            pT = av(6, 2).rearrange("p (k t) -> p k t", t=T)

            def ptr(e):
                for tb in range(TB):
                    for k in range(2):
                        ins = e.transpose(pbf(0)[:, (k * 4 + tb) * 128:(k * 4 + tb + 1) * 128],
                                          pbf_[:, tb, k * 128:(k + 1) * 128], ident[:])
                return ins
            P.op(PE, ptr, reads=ab(4, 2) + [B_const], writes=PB[0])
            P.op(V, lambda e: e.tensor_copy(av(6, 2), pbf(0)), reads=PB[0], writes=ab(6, 2))
            wp4 = load_wp(post_ple_w, av(24, 16, F32), ab(24, 16))
            wple_sb = av(8, 16).rearrange("p (k n) -> p k n", n=D)
            if "ple" not in wkeys:
                wkeys["ple"] = (wsc_alloc(8192), Buf("wk"))
                P.op(G, lambda e: e.dma_start(out=wple_sb, in_=wsrc_ple), writes=ab(8, 16), dma="dw")
                P.op(G, lambda e, off=wkeys["ple"][0]: e.dma_start(out=wsc_ap(off, 8192), in_=av(8, 16)),
                     reads=ab(8, 16), writes=[wkeys["ple"][1]], dma="dx")
            else:
                P.op(G, lambda e, off=wkeys["ple"][0]: e.dma_start(out=av(8, 16), in_=wsc_ap(off, 8192)),
                     reads=[wkeys["ple"][1]], writes=ab(8, 16), dma="dw")
            for cb in range(8):
                wt, wb_ = wload(("pg", cb), wsrc_pg[:, :, cb * 512:(cb + 1) * 512], KC, 512)
                for tb in range(TB):
                    gbk = (cb * TB + tb) % 4
                    pbk = 4 + (cb * TB + tb) % 4

                    def mm(e, wt=wt, tb=tb, gbk=gbk):
                        for kc in range(KC):
                            ins = e.matmul(psum[gbk][:], h3[:, kc, tb * 128:(tb + 1) * 128], wt[:, kc, :],
                                           start=(kc == 0), stop=(kc == KC - 1))
                        return ins
                    P.op(PE, mm, reads=wb_ + hT_tb(tb), writes=PB[gbk])

                    def mm2(e, tb=tb, pbk=pbk, cb=cb):
                        for k in range(2):
                            ins = e.matmul(psum[pbk][:], pT[:, k, tb * 128:(tb + 1) * 128],
                                           wple_sb[:, k, cb * 512:(cb + 1) * 512], start=(k == 0), stop=(k == 1))
                        return ins
                    P.op(PE, mm2, reads=ab(6, 18), writes=PB[pbk])
                    i = evnext()
                    P.op(A, lambda e, i=i, gbk=gbk: e.activation(ev[i][:], psum[gbk][:], AF.Sigmoid), reads=PB[gbk],
                         writes=[EV[i]])
                    P.op(V, lambda e, i=i, pbk=pbk: e.tensor_tensor(ev[i][:], ev[i][:], psum[pbk][:], ALU.mult),
                         reads=([EV[i]] + PB[pbk]), writes=[EV[i]])
                    i2 = evnext()
                    P.op(A, lambda e, i=i, i2=i2, tb=tb, cb=cb: e.activation(ev[i2][:], ev[i][:], AF.Square,
                                                                            accum_out=ssacc[:, tb, cb:cb + 1]),
                         reads=[EV[i]], writes=[EV[i2], B_ss])
                    P.op(V, lambda e, i=i, cb=cb, wp4=wp4: e.tensor_tensor(ev[i][:], ev[i][:],
                                                                           wp4[0][:, cb * 512:(cb + 1) * 512], ALU.mult),
                         reads=[EV[i]] + wp4[1], writes=[EV[i]])
                    P.op(SY, lambda e, i=i, tb=tb, cb=cb, accC=accC: e.dma_start(
                        out=accC(tb)[:, cb * 512:(cb + 1) * 512], in_=ev[i][:]), reads=[EV[i]],
                         writes=accC.bufs(tb), dma="ds")
            if dbg:
                P.op(SY, lambda e, ti=ti: e.dma_start(out=dbgp[ti], in_=pT), reads=ab(6, 2), writes=[Buf("x")], dma="ds")
                P.op(SY, lambda e, ti=ti: e.dma_start(out=dbgw[ti], in_=wple_sb), reads=ab(8, 16), writes=[Buf("x")], dma="ds")
                P.op(SY, lambda e, ti=ti: e.dma_start(out=dbgb[ti], in_=pbf_), reads=ab(4, 2), writes=[Buf("x")], dma="ds")
            def p5(x2rows=x2rows, accC=accC, yrows=yrows):
                n = 0
                bst = B_p5st
                P.op(V, lambda e: e.reduce_sum(st[:, 60:64], ssacc[:], AX.X), reads=[B_ss], writes=[bst])
                rstd_from_ss(st[:, 60:64], D, st[:, 60:64], bst)
                chunks = [(tb, cb) for tb in range(TB) for cb in range(8)]

                def bufs_of(n):
                    k = n % 3
                    return (av(74 + 4 * k, 2, F32), ab(74 + 4 * k, 2), av(76 + 4 * k, 2, F32), ab(76 + 4 * k, 2))

                def load(n):
                    tb, cb = chunks[n]
                    a_ap, a_b, b_ap, b_b = bufs_of(n)
                    cs = slice(cb * 512, (cb + 1) * 512)
                    P.op(SY, lambda e: e.dma_start(out=a_ap, in_=x2rows(tb)[:, cs]), reads=x2rows.bufs(tb),
                         writes=a_b, dma="dl")
                    P.op(SY, lambda e: e.dma_start(out=b_ap, in_=accC(tb)[:, cs]), reads=accC.bufs(tb), writes=b_b,
                         dma="dl")
                load(0)
                load(1)
                for n, (tb, cb) in enumerate(chunks):
                    if n + 2 < len(chunks):
                        load(n + 2)
                    a_ap, a_b, b_ap, b_b = bufs_of(n)
                    cs = slice(cb * 512, (cb + 1) * 512)
                    P.op(V, lambda e, a_ap=a_ap, b_ap=b_ap, tb=tb: e.scalar_tensor_tensor(
                        a_ap, b_ap, st[:, 60 + tb:61 + tb], a_ap, ALU.mult, ALU.add), reads=a_b + b_b + [bst],
                         writes=a_b)
                    P.op(SY, lambda e, a_ap=a_ap, tb=tb, cs=cs: e.dma_start(out=yrows(tb)[:, cs], in_=a_ap),
                         reads=a_b, writes=yrows.bufs(tb), dma="ds")
            if ti == NTILES - 1:
                p5()
            else:
                pending_p5.append(p5)
        if dbg:
            P.op(SY, lambda e: e.dma_start(out=sd, in_=S[:]), reads=B_S, writes=[Buf("x")], dma="ds")

        wflush()
        nc._wkeys = {k: v[0] for k, v in wkeys.items()}
        P.check()

        @block.tensor
        def _(e):
            P.replay("tensor", e, sems)

        @block.vector
        def _(e):
            P.replay("vector", e, sems)

        @block.scalar
        def _(e):
            P.replay("scalar", e, sems)

        @block.gpsimd
        def _(e):
            P.replay("gpsimd", e, sems)

        @block.sync
        def _(e):
            P.replay("sync", e, sems)
            for k in ("ds", "dx"):
                for i in range(DMA_ROT[k]):
                    if P.dcnt.get(f"{k}{i}", 0):
                        e.wait_ge(sems[f"{k}{i}"], P.dcnt[f"{k}{i}"])
    return nc


_NAMES = ["w_in", "w_out", "w_gate", "w_up", "w_down", "w_ple", "w_ple_gate"]


def make_in_maps(inputs, NT, NP, ncores, seq):
    g = lambda k: np.ascontiguousarray(np.asarray(inputs[k], dtype=np.float32))
    x = g("x")
    p = g("p")[0]
    shared = {
        "w_in": g("w_in")[0], "w_out": g("w_out")[0], "w_gate": g("w_gate")[0], "w_up": g("w_up")[0],
        "w_down": g("w_down")[0], "w_ple": g("w_ple")[0], "w_ple_gate": g("w_ple_gate")[0],
        "pre_mix_w": g("pre_mix_w")[0].reshape(32, 128), "pre_ffn_w": g("pre_ffn_w")[0].reshape(32, 128),
        "post_mix_w": g("post_mix_w")[0].reshape(1, D), "post_ffn_w": g("post_ffn_w")[0].reshape(1, D),
        "post_ple_w": g("post_ple_w")[0].reshape(1, D),
        "lb_param": g("lb_param").reshape(32, 128), "a_norm_w": g("a_norm_w")[0].reshape(16, 128),
        "gmlp_ln_w": g("gmlp_ln_w")[0].reshape(1, 2048), "gmlp_ln_b": g("gmlp_ln_b")[0].reshape(1, 2048),
        "w_spatial": g("w_spatial")[0], "b_spatial": g("b_spatial")[0].reshape(1, 2048),
    }
    maps = []
    nhalf = seq // NT
    for c in range(ncores):
        b, hf = divmod(c, nhalf)
        m = dict(shared)
        m["x"] = x[b, hf * NT:(hf + 1) * NT]
        m["p"] = p[b, hf * NT:(hf + 1) * NT]
        if hf == 0 or NP == 0:
            m["xp"] = np.zeros((max(NP, 128), D), np.float32)
        else:
            m["xp"] = x[b, hf * NT - NP:hf * NT]
        maps.append(m)
    return maps


def kernel(**inputs):
    NT, NP = 2048, 2048
    nc = build(NT, NP)
    maps = make_in_maps(inputs, NT, NP, 8, 4096)
    res = run_bass_kernel_spmd(nc, maps, core_ids=list(range(8)))
    out = np.empty((4, 4096, D), np.float32)
    for c in range(8):
        b, hf = divmod(c, 2)
        out[b, hf * NT:(hf + 1) * NT] = res.results[c]["y"]
    return out
```
